# Optimizing a Trainium2 kernel written in Bass

```python
import jax, jax.numpy as jnp
from jax import lax
import numpy as np

D_MODEL = 1024
BATCH = 8
SEQ = 4096
DEPTH = 2

N_A_LAYERS = DEPTH // 2
N_B_LAYERS = DEPTH - N_A_LAYERS
HGRN_EXPAND = 128
HGRN_HEADS = D_MODEL // HGRN_EXPAND
HGRN_DK = HGRN_EXPAND
HGRN_DV = D_MODEL // HGRN_HEADS
HGRN_CHUNK = 64
MLA_HEADS = 16
MLA_NOPE = 128
MLA_ROPE = 64
MLA_V = 128
MLA_Q_LORA = 256
MLA_KV_LORA = 256
ROPE_THETA = 10000.0
QBLOCK = 128
D_FF = 4 * D_MODEL
EPS = 1e-6

kernel_name = 'hybrid_hgrn2_mla_yoco'


def rmsnorm(x, gain):
    xf = x.astype(jnp.float32)
    y = xf * lax.rsqrt(jnp.mean(xf * xf, axis=-1, keepdims=True) + EPS)
    return (y * gain.astype(jnp.float32)).astype(x.dtype)


def rope_tables(seq):
    half = MLA_ROPE // 2
    inv_freq = ROPE_THETA ** (-jnp.arange(half, dtype=jnp.float32) / half)
    ang = jnp.arange(seq, dtype=jnp.float32)[:, None] * inv_freq[None, :]
    return jnp.cos(ang), jnp.sin(ang)


def apply_rope(x, cos, sin):
    half = MLA_ROPE // 2
    xf = x.astype(jnp.float32)
    x1, x2 = xf[..., :half], xf[..., half:]
    return jnp.concatenate([x1 * cos - x2 * sin, x2 * cos + x1 * sin], axis=-1).astype(x.dtype)


def hgrn_lower_bounds(lb_logits):
    return jnp.cumsum(jax.nn.softmax(lb_logits.astype(jnp.float32), axis=0), axis=0)


def hgrn2_mixer(xn, w_q, w_f, w_i, w_g, g_norm, w_o, lb):
    bsz, seq, _ = xn.shape
    nc = seq // HGRN_CHUNK
    f32 = jnp.float32
    q = jax.nn.silu((xn @ w_q).astype(f32))
    forget = lb + (1.0 - lb) * jax.nn.sigmoid((xn @ w_f).astype(f32))
    log_f = jnp.log(forget)
    k = 1.0 - forget
    v = (xn @ w_i).astype(f32)

    def chunks(t, d):
        return t.reshape(bsz, nc, HGRN_CHUNK, HGRN_HEADS, d).transpose(1, 0, 3, 2, 4)

    causal = jnp.tril(jnp.ones((HGRN_CHUNK, HGRN_CHUNK), dtype=bool))

    def step(state, inp):
        qc, kc, vc, gc = inp
        b = jnp.cumsum(gc, axis=2)
        o_inter = jnp.einsum('bhtd,bhdv->bhtv', qc * jnp.exp(b), state)
        diff = b[:, :, :, None, :] - b[:, :, None, :, :]
        decay = jnp.exp(jnp.where(causal[:, :, None], diff, -jnp.inf))
        scores = jnp.einsum('bhtd,bhsd,bhtsd->bhts', qc, kc, decay)
        o_intra = jnp.einsum('bhts,bhsv->bhtv', scores, vc)
        b_last = b[:, :, -1:, :]
        new_state = jnp.exp(b_last[:, :, 0, :])[..., None] * state + jnp.einsum(
            'bhsd,bhsv->bhdv', kc * jnp.exp(b_last - b), vc)
        return new_state, o_inter + o_intra

    state0 = jnp.zeros((bsz, HGRN_HEADS, HGRN_DK, HGRN_DV), f32)
    _, o = lax.scan(step, state0, (chunks(q, HGRN_DK), chunks(k, HGRN_DK),
                                   chunks(v, HGRN_DV), chunks(log_f, HGRN_DK)))
    o = o.transpose(1, 0, 3, 2, 4).reshape(bsz, seq, HGRN_HEADS, HGRN_DV)
    o = rmsnorm(o, g_norm)
    gate = jax.nn.silu((xn @ w_g).astype(f32)).reshape(bsz, seq, HGRN_HEADS, HGRN_DV)
    o = (o * gate).reshape(bsz, seq, D_MODEL).astype(xn.dtype)
    return o @ w_o


def shared_mla_kv(h, in_norm, w_dkv, kv_norm, w_uk, w_uv, cos, sin):
    bsz, seq, _ = h.shape
    hn = rmsnorm(h, in_norm)
    ckr = hn @ w_dkv
    c_kv = rmsnorm(ckr[..., :MLA_KV_LORA], kv_norm)
    k_rope = apply_rope(ckr[..., MLA_KV_LORA:], cos, sin)
    k_nope = (c_kv @ w_uk).reshape(bsz, seq, MLA_HEADS, MLA_NOPE)
    v = (c_kv @ w_uv).reshape(bsz, seq, MLA_HEADS, MLA_V)
    return k_nope, k_rope, v


def mla_mixer(xn, w_dq, q_norm, w_uq, w_o, k_nope, k_rope, v, cos, sin):
    bsz, seq, _ = xn.shape
    nb = seq // QBLOCK
    c_q = rmsnorm(xn @ w_dq, q_norm)
    q = (c_q @ w_uq).reshape(bsz, seq, MLA_HEADS, MLA_NOPE + MLA_ROPE)
    q_nope = q[..., :MLA_NOPE]
    q_rope = apply_rope(q[..., MLA_NOPE:], cos[:, None, :], sin[:, None, :])
    qn_b = q_nope.reshape(bsz, nb, QBLOCK, MLA_HEADS, MLA_NOPE).transpose(1, 0, 2, 3, 4)
    qr_b = q_rope.reshape(bsz, nb, QBLOCK, MLA_HEADS, MLA_ROPE).transpose(1, 0, 2, 3, 4)
    starts = jnp.arange(nb, dtype=jnp.int32) * QBLOCK
    key_pos = jnp.arange(seq, dtype=jnp.int32)
    scale = (MLA_NOPE + MLA_ROPE) ** -0.5

    def block(args):
        qn, qr, start = args
        s = jnp.einsum('bqhd,bkhd->bhqk', qn, k_nope) + jnp.einsum('bqhr,bkr->bhqk', qr, k_rope)
        s = s.astype(jnp.float32) * scale
        q_pos = start + jnp.arange(QBLOCK, dtype=jnp.int32)
        s = jnp.where(key_pos[None, :] <= q_pos[:, None], s, -jnp.inf)
        p = jax.nn.softmax(s, axis=-1).astype(v.dtype)
        return jnp.einsum('bhqk,bkhv->bqhv', p, v)

    o = lax.map(block, (qn_b, qr_b, starts))
    o = o.transpose(1, 0, 2, 3, 4).reshape(bsz, seq, MLA_HEADS * MLA_V)
    return o @ w_o


def sq_relu_mlp(xn, w_up, w_down):
    return jnp.square(jax.nn.relu(xn @ w_up)) @ w_down


def setup_inputs(seed: int = 0) -> dict:
    key = jax.random.key(seed)
    ks = jax.random.split(key, 24)
    f32 = jnp.float32

    def w(k, shape, fan_in):
        return jax.random.normal(k, shape, f32) * (fan_in ** -0.5)

    def gain(k, shape):
        return 1.0 + 0.02 * jax.random.normal(k, shape, f32)

    D = D_MODEL
    return {
        'x': jax.random.normal(ks[0], (BATCH, SEQ, D), f32),
        'hgrn_norm': gain(ks[1], (N_A_LAYERS, D)),
        'hgrn_w_q': w(ks[2], (N_A_LAYERS, D, D), D),
        'hgrn_w_f': w(ks[3], (N_A_LAYERS, D, D), D),
        'hgrn_w_i': w(ks[4], (N_A_LAYERS, D, D), D),
        'hgrn_w_g': w(ks[5], (N_A_LAYERS, D, D), D),
        'hgrn_g_norm': gain(ks[6], (N_A_LAYERS, HGRN_DV)),
        'hgrn_w_o': w(ks[7], (N_A_LAYERS, D, D), D),
        'hgrn_lb_logits': 0.5 * jax.random.normal(ks[8], (N_A_LAYERS + 1, D), f32),
        'mla_norm': gain(ks[9], (N_B_LAYERS, D)),
        'mla_w_dq': w(ks[10], (N_B_LAYERS, D, MLA_Q_LORA), D),
        'mla_q_norm': gain(ks[11], (N_B_LAYERS, MLA_Q_LORA)),
        'mla_w_uq': w(ks[12], (N_B_LAYERS, MLA_Q_LORA, MLA_HEADS * (MLA_NOPE + MLA_ROPE)), MLA_Q_LORA),
        'mla_w_o': w(ks[13], (N_B_LAYERS, MLA_HEADS * MLA_V, D), MLA_HEADS * MLA_V),
        'kv_in_norm': gain(ks[14], (D,)),
        'kv_w_dkv': w(ks[15], (D, MLA_KV_LORA + MLA_ROPE), D),
        'kv_norm': gain(ks[16], (MLA_KV_LORA,)),
        'kv_w_uk': w(ks[17], (MLA_KV_LORA, MLA_HEADS * MLA_NOPE), MLA_KV_LORA),
        'kv_w_uv': w(ks[18], (MLA_KV_LORA, MLA_HEADS * MLA_V), MLA_KV_LORA),
        'mlp_norm': gain(ks[19], (DEPTH, D)),
        'mlp_w_up': w(ks[20], (DEPTH, D, D_FF), D),
        'mlp_w_down': w(ks[21], (DEPTH, D_FF, D), D_FF),
        'final_norm': gain(ks[22], (D,)),
    }


def reference(x, hgrn_norm, hgrn_w_q, hgrn_w_f, hgrn_w_i, hgrn_w_g, hgrn_g_norm, hgrn_w_o,
              hgrn_lb_logits, mla_norm, mla_w_dq, mla_q_norm, mla_w_uq, mla_w_o,
              kv_in_norm, kv_w_dkv, kv_norm, kv_w_uk, kv_w_uv,
              mlp_norm, mlp_w_up, mlp_w_down, final_norm):
    seq = x.shape[1]
    cos, sin = rope_tables(seq)
    lower_bounds = hgrn_lower_bounds(hgrn_lb_logits)
    h = x
    k_nope = k_rope = v = None
    for l in range(DEPTH):
        if l < N_A_LAYERS:
            h = h + hgrn2_mixer(rmsnorm(h, hgrn_norm[l]), hgrn_w_q[l], hgrn_w_f[l], hgrn_w_i[l],
                                hgrn_w_g[l], hgrn_g_norm[l], hgrn_w_o[l], lower_bounds[l])
        else:
            j = l - N_A_LAYERS
            h = h + mla_mixer(rmsnorm(h, mla_norm[j]), mla_w_dq[j], mla_q_norm[j], mla_w_uq[j],
                              mla_w_o[j], k_nope, k_rope, v, cos, sin)
        h = h + sq_relu_mlp(rmsnorm(h, mlp_norm[l]), mlp_w_up[l], mlp_w_down[l])
        if l == N_A_LAYERS - 1:
            k_nope, k_rope, v = shared_mla_kv(h, kv_in_norm, kv_w_dkv, kv_norm, kv_w_uk, kv_w_uv, cos, sin)
    return rmsnorm(h, final_norm)
```

```python
import numpy as np
import ml_dtypes
from contextlib import ExitStack
import concourse.bass as bass
import concourse.mybir as mybir
from concourse.bass_utils import run_bass_kernel_spmd

F32 = mybir.dt.float32
BF16 = mybir.dt.bfloat16
AF = mybir.ActivationFunctionType
ALU = mybir.AluOpType

P = 128
D = 1024
DFF = 4096
T = 512
NSUB = 4
EPS = 1e-6
NH = 16
ATT_SCALE = float((128 + 64) ** -0.5)
SEM_LIMIT = 12000


class Ev:
    __slots__ = ("sem", "val", "eng", "key")

    def __init__(self, eng):
        self.sem = None
        self.val = None
        self.eng = eng
        self.key = None


class Slot:
    __slots__ = ("name", "writers", "readers")

    def __init__(self, name=""):
        self.name = name
        self.writers = []
        self.readers = []


class Eng:
    def __init__(self, prog, name):
        self.prog = prog
        self.name = name
        self.ops = []
        self.pending = []
        self.sem = None
        self.semkey = None
        self.cnt = 0
        self.waited = {}
        self.dma_sems = []
        self.dma_cnt = []
        self.dma_rr = 0

    def _new_sem(self):
        self.sem, self.semkey = self.prog.new_sem(self.name)
        self.cnt = 0

    def op(self, fn, reads=(), writes=(), kind="inc"):
        ev = Ev(self if kind != "dma" else None)
        waits = []
        for sl in reads:
            waits.extend(sl.writers)
        same_ok = (self.name == "pe")
        for sl in writes:
            for e in sl.readers:
                if e.eng is not self or not same_ok:
                    waits.append(e)
            for e in sl.writers:
                if e.eng is not self or not same_ok:
                    waits.append(e)
        for sl in reads:
            sl.readers.append(ev)
        for sl in writes:
            if sl.readers:
                sl.writers = [ev]
                sl.readers = []
            else:
                sl.writers.append(ev)
        if kind == "inc":
            if self.sem is None or self.cnt >= SEM_LIMIT:
                self._new_sem()
            self.cnt += 1
            ev.sem, ev.key, ev.val = self.sem, self.semkey, self.cnt
            for pe in self.pending:
                pe.sem, pe.key, pe.val = ev.sem, ev.key, ev.val
            self.pending = []
        elif kind == "dma":
            if not self.dma_sems:
                for i in range(self.prog.nds):
                    s, k = self.prog.new_sem(self.name + "_dma%d" % i)
                    self.dma_sems.append((s, k))
                    self.dma_cnt.append(0)
            j = self.dma_rr % len(self.dma_sems)
            self.dma_rr += 1
            s, k = self.dma_sems[j]
            if self.dma_cnt[j] > 0:
                pw = Ev(None)
                pw.sem, pw.key, pw.val = s, k, self.dma_cnt[j]
                waits.append(pw)
            self.dma_cnt[j] += 16
            ev.sem, ev.key, ev.val = s, k, self.dma_cnt[j]
        else:
            self.pending.append(ev)
        self.ops.append((waits, fn, ev, kind))
        return ev

    def emit(self, e):
        for (waits, fn, ev, kind) in self.ops:
            for w in waits:
                assert w.sem is not None, "unresolved event"
                if self.waited.get(w.key, 0) < w.val:
                    e.wait_ge(w.sem, w.val)
                    self.waited[w.key] = w.val
            ins = fn(e)
            if kind == "inc":
                ins.then_inc(ev.sem, 1)
            elif kind == "dma":
                ins.then_inc(ev.sem, 16)
        self.ops = []


class Prog:
    def __init__(self, nc, es, nds=12):
        self.nc = nc
        self.es = es
        self.nds = nds
        self.nsem = 0
        self.pe = Eng(self, "pe")
        self.act = Eng(self, "act")
        self.dve = Eng(self, "dve")
        self.pool = Eng(self, "pool")
        self.sp = Eng(self, "sp")

    def new_sem(self, name):
        self.nsem += 1
        s = self.es.enter_context(self.nc.semaphore("s%d_%s" % (self.nsem, name)))
        return s, self.nsem

    def flush(self, final_waits=()):
        nc = self.nc
        for en in (self.pe, self.act, self.dve, self.pool, self.sp):
            assert not en.pending, "pending events on %s" % en.name
        with nc.Block() as block:
            if self.pe.ops:
                @block.tensor
                def _(e):
                    self.pe.emit(e)
            if self.act.ops:
                @block.scalar
                def _(e):
                    self.act.emit(e)
            if self.dve.ops:
                @block.vector
                def _(e):
                    self.dve.emit(e)
            if True:
                @block.gpsimd
                def _(e):
                    self.pool.emit(e)
                    for w in final_waits:
                        e.wait_ge(w.sem, w.val)
                    for en in (self.sp, self.pool):
                        for (s, k), c in zip(en.dma_sems, en.dma_cnt):
                            if c > 0:
                                e.wait_ge(s, c)
            if self.sp.ops:
                @block.sync
                def _(e):
                    self.sp.emit(e)


class Buf:
    __slots__ = ("t", "s")

    def __init__(self, t, name=""):
        self.t = t
        self.s = Slot(name)


def _rope_tables(S):
    half = 32
    inv = (np.float32(10000.0) ** (-(np.arange(half, dtype=np.float32) / np.float32(half)))).astype(np.float32)
    pos = np.arange(S, dtype=np.float32)
    ang = (pos[:, None] * inv[None, :]).astype(np.float32)
    cos = np.cos(ang).astype(np.float32)
    sin = np.sin(ang).astype(np.float32)
    cos2 = np.concatenate([cos, cos], axis=1)
    sins = np.concatenate([-sin, sin], axis=1)
    return cos2, sins


def _consts(S):
    c = {}
    c["ident"] = np.eye(P, dtype=np.float32).astype(ml_dtypes.bfloat16)
    s = np.arange(P)[:, None]
    t = np.arange(P)[None, :]
    c["bdmask"] = (((s // 64) == (t // 64)) & (s <= t)).astype(np.float32)
    c["trimask"] = (s <= t).astype(np.float32).astype(ml_dtypes.bfloat16)
    c["ones"] = np.ones((P, P), dtype=np.float32).astype(ml_dtypes.bfloat16)
    c["onesdiv"] = np.full((P, P), 1.0 / 128.0, dtype=np.float32).astype(ml_dtypes.bfloat16)
    cm = np.ones((P, T), dtype=np.float32)
    cm[:, ::64] = 0.0
    c["cmask"] = cm
    cos2, sins = _rope_tables(S)
    c["cos_fm"] = np.ascontiguousarray(np.concatenate([cos2.T, cos2.T], axis=0))
    c["sin_fm"] = np.ascontiguousarray(np.concatenate([sins.T, sins.T], axis=0))
    c["ones32"] = np.ones((P, P), dtype=np.float32)
    c["cos_tm"] = np.ascontiguousarray(cos2.reshape(S // P, P, 64).transpose(1, 0, 2))
    c["sin_tm"] = np.ascontiguousarray(sins.reshape(S // P, P, 64).transpose(1, 0, 2))
    return c


def _chunkify(W):
    R, C = W.shape
    out = []
    for j in range(C // 512):
        for i in range(R // 1024):
            blk = W[i * 1024:(i + 1) * 1024, j * 512:(j + 1) * 512]
            out.append(blk.reshape(8, P, 512).transpose(1, 0, 2).reshape(P, 4096))
    return out


def _fm(v):
    v = np.asarray(v, dtype=np.float32).reshape(-1, P)
    return v.T


CH_F, CH_Q, CH_I, CH_G, CH_O = 0, 2, 4, 6, 8
CH_UP0, CH_DN0 = 10, 18
CH_WO1 = 26
CH_UP1, CH_DN1 = 30, 38
NCH = 46
G_HGRN, G_MLP0, G_MLP1, G_KVIN, G_MLA = 0, 8, 16, 24, 32
G_KVN, G_QN, G_GN, G_L0, G_L1 = 40, 42, 44, 45, 53
NG = 61


def _pack_weights(inp):
    chunks = []
    for nm in ("hgrn_w_f", "hgrn_w_q", "hgrn_w_i", "hgrn_w_g", "hgrn_w_o"):
        chunks += _chunkify(inp[nm][0])
    chunks += _chunkify(inp["mlp_w_up"][0])
    chunks += _chunkify(inp["mlp_w_down"][0])
    chunks += _chunkify(inp["mla_w_o"][0])
    chunks += _chunkify(inp["mlp_w_up"][1])
    chunks += _chunkify(inp["mlp_w_down"][1])
    assert len(chunks) == NCH
    wbig = np.ascontiguousarray(np.stack(chunks, axis=0), dtype=np.float32)

    wd = inp["kv_w_dkv"]
    wdkv = np.concatenate([wd, wd[:, 288:320], wd[:, 256:288]], axis=1)
    wdkv = wdkv.reshape(8, P, 384).transpose(1, 0, 2)
    wdq = inp["mla_w_dq"][0].reshape(8, P, 256).transpose(1, 0, 2)
    wsm = np.ascontiguousarray(np.concatenate([wdkv, wdq], axis=2), dtype=np.float32)

    uk, uv, uq = inp["kv_w_uk"], inp["kv_w_uv"], inp["mla_w_uq"][0]
    wh = np.empty((NH, 256, 640), dtype=np.float32)
    for h in range(NH):
        wh[h, :, 0:128] = uk[:, h * 128:(h + 1) * 128]
        wh[h, :, 128:256] = uv[:, h * 128:(h + 1) * 128]
        wh[h, :, 256:384] = uq[:, h * 192:h * 192 + 128]
        rope = uq[:, h * 192 + 128:h * 192 + 192]
        swap = np.concatenate([uq[:, h * 192 + 160:h * 192 + 192], uq[:, h * 192 + 128:h * 192 + 160]], axis=1)
        wh[h, :, 384:448] = rope
        wh[h, :, 448:512] = rope
        wh[h, :, 512:576] = swap
        wh[h, :, 576:640] = swap
    wh = np.ascontiguousarray(wh.reshape(NH, 2, P, 640).transpose(0, 2, 1, 3))

    g = np.empty((P, NG), dtype=np.float32)
    g[:, G_HGRN:G_HGRN + 8] = _fm(inp["hgrn_norm"][0])
    g[:, G_MLP0:G_MLP0 + 8] = _fm(inp["mlp_norm"][0])
    g[:, G_MLP1:G_MLP1 + 8] = _fm(inp["mlp_norm"][1])
    g[:, G_KVIN:G_KVIN + 8] = _fm(inp["kv_in_norm"])
    g[:, G_MLA:G_MLA + 8] = _fm(inp["mla_norm"][0])
    g[:, G_KVN:G_KVN + 2] = _fm(inp["kv_norm"])
    g[:, G_QN:G_QN + 2] = _fm(inp["mla_q_norm"][0])
    g[:, G_GN:G_GN + 1] = _fm(inp["hgrn_g_norm"][0])
    g[:, G_L0:G_L0 + 8] = _fm(inp["hgrn_lb_logits"][0])
    g[:, G_L1:G_L1 + 8] = _fm(inp["hgrn_lb_logits"][1])
    fn = np.ascontiguousarray(np.broadcast_to(inp["final_norm"].astype(np.float32)[None, :], (P, D)))
    return {"wbig": wbig, "wsm": wsm, "wh": wh, "gains": g, "fnorm": fn}


def build(S=4096, upto="C", debug=False):
    import os as _os
    nc = bass.Bass("TRN2", target_bir_lowering=False)
    NT = S // T
    NTT = S // P
    NTB = S // 512
    es = ExitStack()
    pg = Prog(nc, es)
    PE, ACT, DVE, POOL, SP = pg.pe, pg.act, pg.dve, pg.pool, pg.sp
    dbgset = set(debug.split(",")) if isinstance(debug, str) else set()

    def mm(out, lhsT, rhs, start, stop, reads, writes, last=True, skip=False):
        return PE.op(lambda e: e.matmul(out, lhsT, rhs, start=start, stop=stop, skip_group_check=skip),
                     reads=reads, writes=writes, kind=("inc" if last else "noinc"))

    def tp(out, in_, reads, writes, last=True):
        idn = ident.t[0:in_.shape[0], 0:in_.shape[0]]
        return PE.op(lambda e: e.transpose(out=out, in_=in_, identity=idn),
                     reads=list(reads) + [ident.s], writes=writes, kind=("inc" if last else "noinc"))

    def act(out, in_, func, reads, writes, scale=None, bias=None, accum=None):
        kw = {}
        if scale is not None:
            kw["scale"] = scale
        if bias is not None:
            kw["bias"] = bias
        if accum is not None:
            kw["accum_out"] = accum
        return ACT.op(lambda e: e.activation(out=out, in_=in_, func=func, **kw), reads=reads, writes=writes)

    def tt(E, out, in0, in1, op, reads, writes):
        return E.op(lambda e: e.tensor_tensor(out=out, in0=in0, in1=in1, op=op), reads=reads, writes=writes)

    def ts(E, out, in0, s1, s2, op0, op1, reads, writes):
        if s2 is None:
            return E.op(lambda e: e.tensor_scalar(out=out, in0=in0, scalar1=s1, scalar2=None, op0=op0),
                        reads=reads, writes=writes)
        return E.op(lambda e: e.tensor_scalar(out=out, in0=in0, scalar1=s1, scalar2=s2, op0=op0, op1=op1),
                    reads=reads, writes=writes)

    def stt(out, in0, scalar, in1, op0, op1, reads, writes):
        return DVE.op(lambda e: e.scalar_tensor_tensor(out=out, in0=in0, scalar=scalar, in1=in1, op0=op0, op1=op1),
                      reads=reads, writes=writes)

    def cp(E, out, in_, reads, writes):
        return E.op(lambda e: e.tensor_copy(out=out, in_=in_), reads=reads, writes=writes)

    def mset(E, ap, val, writes):
        return E.op(lambda e: e.memset(ap, val), writes=writes)

    def dma(Q, out, in_, reads, writes):
        return Q.op(lambda e: e.dma_start(out=out, in_=in_), reads=reads, writes=writes, kind="dma")

    def dram(name, shape, dt, kind):
        return nc.dram_tensor(name, list(shape), dt, kind=kind)

    x_d = dram("x", [S, D], F32, "ExternalInput")
    wbig_d = dram("wbig", [NCH, P, 4096], F32, "ExternalInput")
    wsm_d = dram("wsm", [P, 8, 640], F32, "ExternalInput")
    wh_d = dram("wh", [NH, P, 2, 640], F32, "ExternalInput")
    gains_d = dram("gains", [P, NG], F32, "ExternalInput")
    fnorm_d = dram("fnorm", [P, D], F32, "ExternalInput")
    ident_d = dram("ident", [P, P], BF16, "ExternalInput")
    bdmask_d = dram("bdmask", [P, P], F32, "ExternalInput")
    trimask_d = dram("trimask", [P, P], BF16, "ExternalInput")
    ones_d = dram("ones", [P, P], BF16, "ExternalInput")
    onesdiv_d = dram("onesdiv", [P, P], BF16, "ExternalInput")
    cmask_d = dram("cmask", [P, T], F32, "ExternalInput")
    cosfm_d = dram("cos_fm", [P, S], F32, "ExternalInput")
    sinfm_d = dram("sin_fm", [P, S], F32, "ExternalInput")
    ones32_d = dram("ones32", [P, P], F32, "ExternalInput")
    costm_d = dram("cos_tm", [P, NTT, 64], F32, "ExternalInput")
    sintm_d = dram("sin_tm", [P, NTT, 64], F32, "ExternalInput")
    out_d = dram("out", [S, D], F32, "ExternalOutput")
    wb_d = dram("wb_scr", [NCH, P, 4096], BF16, "Internal")
    h1_d = dram("h1_scr", [S, D], F32, "ExternalOutput" if "h1" in dbgset else "Internal")
    o_d = dram("o_scr", [NH, P, S], BF16, "ExternalOutput" if "o" in dbgset else "Internal")
    wb_slots = [Slot("wb%d" % i) for i in range(NCH)]
    h1_slots = [Slot("h1_%d" % i) for i in range(NT)]
    o_slots = [[Slot("o_%d_%d" % (h, i)) for i in range(NTB)] for h in range(NH)]
    out_evs = []

    def sb(name, shape, dt, scope):
        return scope.enter_context(nc.sbuf_tensor("sb_" + name, list(shape), dt))

    ps = [Buf(es.enter_context(nc.psum_tensor("ps%d" % i, [P, 512], F32)), "ps%d" % i) for i in range(8)]
    psb = [p_.t[:, :].bitcast(BF16) for p_ in ps]

    ident = Buf(sb("ident", [P, P], BF16, es))
    bdmask = Buf(sb("bdmask", [P, P], F32, es))
    trimask = Buf(sb("trimask", [P, P], BF16, es))
    ones = Buf(sb("ones", [P, P], BF16, es))
    onesdiv = Buf(sb("onesdiv", [P, P], BF16, es))
    gains = Buf(sb("gains", [P, NG], F32, es))
    lbt = Buf(sb("lbt", [P, 32], F32, es))
    epsb = Buf(sb("epsb", [P, 1], F32, es))
    ckvT = Buf(sb("ckvT", [P, 2, S], BF16, es))
    cqT = Buf(sb("cqT", [P, 2, S], BF16, es))
    kropeT = Buf(sb("kropeT", [P, S], BF16, es))
    ones32 = Buf(sb("ones32", [P, P], F32, es))

    dma(SP, ident.t[:, :], ident_d.ap(), [], [ident.s])
    dma(SP, bdmask.t[:, :], bdmask_d.ap(), [], [bdmask.s])
    dma(SP, trimask.t[:, :], trimask_d.ap(), [], [trimask.s])
    dma(SP, ones.t[:, :], ones_d.ap(), [], [ones.s])
    dma(SP, onesdiv.t[:, :], onesdiv_d.ap(), [], [onesdiv.s])
    dma(SP, gains.t[:, :], gains_d.ap(), [], [gains.s])
    dma(SP, ones32.t[:, :], ones32_d.ap(), [], [ones32.s])
    mset(POOL, epsb.t[:, :], EPS, [epsb.s])
    tt(DVE, lbt.t[:, 24:32], gains.t[:, G_L1:G_L1 + 8], gains.t[:, G_L0:G_L0 + 8], ALU.subtract, [gains.s], [lbt.s])
    act(lbt.t[:, 24:32], lbt.t[:, 24:32], AF.Exp, [lbt.s], [lbt.s])
    act(lbt.t[:, 24:32], lbt.t[:, 24:32], AF.Ln, [lbt.s], [lbt.s], bias=1.0)
    act(lbt.t[:, 0:8], lbt.t[:, 24:32], AF.Exp, [lbt.s], [lbt.s], scale=-1.0)
    ts(DVE, lbt.t[:, 8:16], lbt.t[:, 0:8], -1.0, 1.0, ALU.mult, ALU.add, [lbt.s], [lbt.s])

    for c in range(int(_os.environ.get("DBG_NCH", NCH))):
        dma(POOL, wb_d.ap()[c], wbig_d.ap()[c], [], [wb_slots[c]])
    if upto == "0":
        pg.flush(final_waits=[w for sl in wb_slots for w in sl.writers])
        return nc, es

    class HS:
        def __init__(self, t, name):
            self.t = t
            self.s = [Slot("%s_%d" % (name, i)) for i in range(NSUB)]

    def rmsnorm_T(nscope, hs, sub, dsts):
        junk, ss, xh, pbi = nscope
        hsl = hs.t[:, sub, :]
        act(junk.t[:, :], hsl, AF.Square, [hs.s[sub]], [junk.s, ss.s], accum=ss.t[:, 0:1])
        act(ss.t[:, 1:2], ss.t[:, 0:1], AF.Ln, [ss.s, epsb.s], [ss.s], scale=1.0 / D, bias=epsb.t[:, 0:1])
        act(ss.t[:, 2:3], ss.t[:, 1:2], AF.Exp, [ss.s], [ss.s], scale=-0.5)
        ts(DVE, xh.t[:, :], hsl, ss.t[:, 2:3], None, ALU.mult, None, [hs.s[sub], ss.s], [xh.s])
        pb = psb[pbi]
        for kc in range(8):
            tp(pb[:, kc * P:(kc + 1) * P], xh.t[:, kc * P:(kc + 1) * P], [xh.s], [ps[pbi].s], last=(kc == 7))
        for (dst3, dslots, gcol) in dsts:
            gb = gains.t[:, gcol:gcol + 8].unsqueeze(2).to_broadcast([P, 8, P])
            tt(DVE, dst3[:, :, sub * P:(sub + 1) * P], pb.rearrange("p (k t) -> p k t", t=P), gb, ALU.mult,
               [ps[pbi].s, gains.s], dslots)

    wrr = [0]

    def wload(wbufs, c):
        b = wbufs[wrr[0] % len(wbufs)]
        wrr[0] += 1
        dma(SP, b.t[:, :], wb_d.ap()[c], [wb_slots[c]], [b.s])
        return b

    def mlp(hs, xnT, actT, act_slots, nscope, wbufs, sqb, ch_up, ch_dn, gcol, pbanks_up, pbanks_dn):
        for sub in range(NSUB):
            rmsnorm_T(nscope, hs, sub, [(xnT.t, [xnT.s], gcol)])
        k = 0
        for j in range(8):
            wb = wload(wbufs, ch_up + j)
            for fl in range(4):
                fb = j * 4 + fl
                pb = ps[pbanks_up[k % len(pbanks_up)]]
                sq = sqb[k % len(sqb)]
                k += 1
                for kc in range(8):
                    mm(pb.t[:, :], wb.t[:, kc * 512 + fl * P: kc * 512 + (fl + 1) * P], xnT.t[:, kc, :],
                       kc == 0, kc == 7, [wb.s, xnT.s], [pb.s], last=(kc == 7))
                act(sq.t[:, :], pb.t[:, :], AF.Square, [pb.s], [sq.s])
                stt(actT[:, fb, :], pb.t[:, :], 0.0, sq.t[:, :], ALU.is_gt, ALU.mult, [pb.s, sq.s], act_slots)
        for hh in range(2):
            for i in range(4):
                wb = wload(wbufs, ch_dn + hh * 4 + i)
                for sub in range(NSUB):
                    pb = ps[pbanks_dn[sub]]
                    for kc in range(8):
                        mm(pb.t[:, :], actT[:, i * 8 + kc, sub * P:(sub + 1) * P], wb.t[:, kc * 512:(kc + 1) * 512],
                           (i == 0 and kc == 0), (i == 3 and kc == 7), [wb.s] + act_slots, [pb.s], last=(kc == 7))
            for sub in range(NSUB):
                pb = ps[pbanks_dn[sub]]
                hv = hs.t[:, sub, hh * 512:(hh + 1) * 512]
                tt(DVE, hv, hv, pb.t[:, :], ALU.add, [pb.s, hs.s[sub]], [hs.s[sub]])

    def sigmoid_chain(pz, buf):
        act(buf.t[:, :], pz.t[:, :], AF.Exp, [pz.s], [buf.s], scale=-1.0)
        act(buf.t[:, :], buf.t[:, :], AF.Ln, [buf.s], [buf.s], bias=1.0)
        act(buf.t[:, :], buf.t[:, :], AF.Exp, [buf.s], [buf.s], scale=-1.0)

    with ExitStack() as sa:
        hs = HS(sb("hs", [P, NSUB, D], F32, sa), "hs")
        xnT = Buf(sb("xnT", [P, 8, T], BF16, sa))
        junk = Buf(sb("junk", [P, D], BF16, sa))
        ss = Buf(sb("ss", [P, 4], F32, sa))
        xh = Buf(sb("xh", [P, D], BF16, sa))
        wbufs = [Buf(sb("wbuf%d" % i, [P, 4096], BF16, sa)) for i in range(3)]
        wsm = Buf(sb("wsm", [P, 8, 640], BF16, sa))
        cmask = Buf(sb("cmask", [P, T], F32, sa))
        costm = Buf(sb("costm", [P, NSUB, 64], F32, sa))
        sintm = Buf(sb("sintm", [P, NSUB, 64], F32, sa))
        ef = [Buf(sb("ef%d" % i, [P, T], F32, sa)) for i in range(2)]
        eq = [Buf(sb("eq%d" % i, [P, T], F32, sa)) for i in range(3)]
        l1 = Buf(sb("l1", [P, T], F32, sa))
        l2 = Buf(sb("l2", [P, T], F32, sa))
        sf = Buf(sb("sf", [P, T], F32, sa))
        kk = Buf(sb("kk", [P, T], F32, sa))
        bb = Buf(sb("bb", [P, T], F32, sa))
        eb = Buf(sb("eb", [P, T], F32, sa))
        enb = Buf(sb("enb", [P, T], F32, sa))
        qs = Buf(sb("qs", [P, T], F32, sa))
        kt32 = Buf(sb("kt32", [P, T], F32, sa))
        ebl = Buf(sb("ebl", [P, 8, 8], F32, sa))
        big = sb("big", [P, 32 * T], BF16, sa)
        qtT = Buf(big[:, 0:8 * T].rearrange("p (k t) -> p k t", t=T))
        ktT = Buf(big[:, 8 * T:16 * T].rearrange("p (k t) -> p k t", t=T))
        khT = Buf(big[:, 16 * T:24 * T].rearrange("p (k t) -> p k t", t=T))
        kh = Buf(big[:, 24 * T:32 * T].rearrange("p (s d) -> p s d", d=D))
        actT = big[:, :].rearrange("p (k t) -> p k t", t=T)
        act_slots = [qtT.s, ktT.s, khT.s, kh.s]
        vtm = Buf(sb("vtm", [P, NSUB, D], BF16, sa))
        gateT = Buf(sb("gateT", [P, 8, T], BF16, sa))
        ogT = Buf(sb("ogT", [P, 8, T], BF16, sa))
        S32t = sb("S32", [P, 8, P], F32, sa)
        Sbft = sb("Sbf", [P, 8, P], BF16, sa)
        S32s = [Slot("S32_%d" % i) for i in range(8)]
        Sbfs = [Slot("Sbf_%d" % i) for i in range(8)]
        scm = [Buf(sb("scm%d" % i, [P, 4, P], BF16, sa)) for i in range(2)]
        osq = Buf(sb("osq", [P, T], BF16, sa))
        ckn = Buf(sb("ckn", [P, 256], BF16, sa))
        cqn = Buf(sb("cqn", [P, 256], BF16, sa))
        krb = Buf(sb("krb", [P, P], BF16, sa))
        rt1 = Buf(sb("rt1", [P, 64], F32, sa))
        rt2 = Buf(sb("rt2", [P, 64], F32, sa))
        ss2 = Buf(sb("ss2", [P, 6], F32, sa))
        nscope = (junk, ss, xh, 7)

        dma(SP, cmask.t[:, :], cmask_d.ap(), [], [cmask.s])
        dma(POOL, wsm.t[:, :, :], wsm_d.ap(), [], [wsm.s])
        mset(POOL, S32t[:, :, :], 0.0, S32s)
        mset(POOL, Sbft[:, :, :], 0.0, Sbfs)

        def proj_fm(wb, hl, pb):
            for kc in range(8):
                mm(pb.t[:, :], wb.t[:, kc * 512 + hl * P: kc * 512 + (hl + 1) * P], xnT.t[:, kc, :],
                   kc == 0, kc == 7, [wb.s, xnT.s], [pb.s], last=(kc == 7))

        hcount = 0
        _stop = int(_os.environ.get('DBG_STOP', 99))
        for ti in range(NT):
            t0 = ti * T
            for sub in range(NSUB):
                dma(POOL, hs.t[:, sub, :], x_d.ap()[t0 + sub * P: t0 + (sub + 1) * P, :], [], [hs.s[sub]])
            dma(SP, costm.t[:, :, :], costm_d.ap()[:, ti * NSUB:(ti + 1) * NSUB, :], [], [costm.s])
            dma(SP, sintm.t[:, :, :], sintm_d.ap()[:, ti * NSUB:(ti + 1) * NSUB, :], [], [sintm.s])
            for sub in range(NSUB):
                rmsnorm_T(nscope, hs, sub, [(xnT.t, [xnT.s], G_HGRN)])

            if _stop <= 1:
                break
            for c in range(2):
                wf = wload(wbufs, CH_F + c)
                wq = wload(wbufs, CH_Q + c)
                for hl in range(4):
                    hd = c * 4 + hl
                    pf, pq = ps[(2 * hd) % 6], ps[(2 * hd + 1) % 6]
                    ef_ = ef[hcount % 2]
                    eq_ = eq[hcount % 3]
                    hcount += 1
                    proj_fm(wf, hl, pf)
                    proj_fm(wq, hl, pq)
                    lb_ap = lbt.t[:, hd:hd + 1]
                    oml_ap = lbt.t[:, 8 + hd:9 + hd]
                    act(ef_.t[:, :], pf.t[:, :], AF.Exp, [pf.s], [ef_.s], scale=-1.0)
                    act(l1.t[:, :], ef_.t[:, :], AF.Ln, [ef_.s], [l1.s], bias=1.0)
                    act(l2.t[:, :], ef_.t[:, :], AF.Ln, [ef_.s, lbt.s], [l2.s], scale=lb_ap, bias=1.0)
                    act(sf.t[:, :], l1.t[:, :], AF.Exp, [l1.s], [sf.s], scale=-1.0)
                    tt(DVE, l2.t[:, :], l2.t[:, :], l1.t[:, :], ALU.subtract, [l1.s, l2.s], [l2.s])
                    stt(kk.t[:, :], ef_.t[:, :], oml_ap, sf.t[:, :], ALU.mult, ALU.mult, [ef_.s, lbt.s, sf.s], [kk.s])
                    DVE.op(lambda e: e.tensor_tensor_scan(out=bb.t[:, :], data0=cmask.t[:, :], data1=l2.t[:, :],
                                                          initial=0.0, op0=ALU.mult, op1=ALU.add),
                           reads=[l2.s, cmask.s], writes=[bb.s])
                    act(eb.t[:, :], bb.t[:, :], AF.Exp, [bb.s], [eb.s])
                    act(enb.t[:, :], bb.t[:, :], AF.Exp, [bb.s], [enb.s], scale=-1.0)
                    sigmoid_chain(pq, eq_)
                    tt(DVE, qs.t[:, :], pq.t[:, :], eq_.t[:, :], ALU.mult, [pq.s, eq_.s], [qs.s])
                    tt(DVE, qtT.t[:, hd, :], qs.t[:, :], eb.t[:, :], ALU.mult, [qs.s, eb.s], [qtT.s])
                    tt(DVE, kt32.t[:, :], kk.t[:, :], enb.t[:, :], ALU.mult, [kk.s, enb.s], [kt32.s])
                    cp(POOL, ktT.t[:, hd, :], kt32.t[:, :], [kt32.s], [ktT.s])
                    ebv = eb.t[:, :].rearrange("p (c t) -> p c t", t=64)[:, :, 63:64]
                    cp(POOL, ebl.t[:, hd, :].unsqueeze(2), ebv, [eb.s], [ebl.s])
                    tt(POOL, khT.t[:, hd, :].rearrange("p (c t) -> p c t", t=64),
                       kt32.t[:, :].rearrange("p (c t) -> p c t", t=64), ebv.to_broadcast([P, 8, 64]), ALU.mult,
                       [kt32.s, eb.s], [khT.s])
            if _stop <= 2:
                break
            for c in range(2):
                wg = wload(wbufs, CH_G + c)
                for hl in range(4):
                    hd = c * 4 + hl
                    pb = ps[hd % 6]
                    eq_ = eq[hcount % 3]
                    hcount += 1
                    proj_fm(wg, hl, pb)
                    sigmoid_chain(pb, eq_)
                    tt(DVE, gateT.t[:, hd, :], pb.t[:, :], eq_.t[:, :], ALU.mult, [pb.s, eq_.s], [gateT.s])

            if _stop <= 3:
                break
            for sub in range(NSUB):
                bi = 6 + (sub % 2)
                for hd in range(8):
                    tp(psb[bi][:, hd * P:(hd + 1) * P], khT.t[:, hd, sub * P:(sub + 1) * P], [khT.s], [ps[bi].s],
                       last=(hd == 7))
                act(kh.t[:, sub, :], psb[bi], AF.Copy, [ps[bi].s], [kh.s])

            if _stop <= 4:
                break
            for c in range(2):
                wi = wload(wbufs, CH_I + c)
                for sub in range(NSUB):
                    pb = ps[4 + (sub % 2)]
                    for kc in range(8):
                        mm(pb.t[:, :], xnT.t[:, kc, sub * P:(sub + 1) * P], wi.t[:, kc * 512:(kc + 1) * 512],
                           kc == 0, kc == 7, [wi.s, xnT.s], [pb.s], last=(kc == 7))
                    cp(DVE, vtm.t[:, sub, c * 512:(c + 1) * 512], pb.t[:, :], [pb.s], [vtm.s])

            if _stop <= 5:
                break
            for prr in range(NSUB):
                c0 = prr * P
                for g in range(2):
                    mset(DVE, ps[2 + g].t[:, :], 0.0, [ps[2 + g].s])
                for g in range(2):
                    for hl in range(4):
                        hd = g * 4 + hl
                        mm(ps[g].t[:, hl * P:(hl + 1) * P], ktT.t[:, hd, c0:c0 + P], qtT.t[:, hd, c0:c0 + P],
                           True, True, [ktT.s, qtT.s], [ps[g].s], last=(hl == 3))
                    tt(DVE, scm[g].t[:, :, :], ps[g].t[:, :].rearrange("p (h t) -> p h t", t=P),
                       bdmask.t[:, :].unsqueeze(1).to_broadcast([P, 4, P]), ALU.mult, [ps[g].s, bdmask.s], [scm[g].s])
                for half in range(2):
                    r0 = half * 64
                    cc0 = c0 + r0
                    ch = prr * 2 + half
                    for g in range(2):
                        po, pu = ps[2 + g], ps[4 + g]
                        for hl in range(4):
                            hd = g * 4 + hl
                            if half == 0:
                                mm(po.t[:, hl * P:(hl + 1) * P], vtm.t[:, prr, hd * P:(hd + 1) * P], scm[g].t[:, hl, :],
                                   False, False, [vtm.s, scm[g].s], [po.s], last=False, skip=True)
                            mm(po.t[:, hl * P + r0: hl * P + r0 + 64], Sbft[:, hd, :], qtT.t[:, hd, cc0:cc0 + 64],
                               False, (half == 1), [Sbfs[hd], qtT.s], [po.s], last=False, skip=True)
                            mm(pu.t[:, hl * P:(hl + 1) * P], kh.t[r0:r0 + 64, prr, hd * P:(hd + 1) * P],
                               vtm.t[r0:r0 + 64, prr, hd * P:(hd + 1) * P], True, True, [kh.s, vtm.s], [pu.s],
                               last=(hl == 3))
                        for hl in range(4):
                            hd = g * 4 + hl
                            stt(S32t[:, hd, :], S32t[:, hd, :], ebl.t[:, hd, ch:ch + 1], pu.t[:, hl * P:(hl + 1) * P],
                                ALU.mult, ALU.add, [S32s[hd], ebl.s, pu.s], [S32s[hd]])
                        cp(POOL, Sbft[:, g * 4:(g + 1) * 4, :], S32t[:, g * 4:(g + 1) * 4, :],
                           S32s[g * 4:(g + 1) * 4], Sbfs[g * 4:(g + 1) * 4])
                for g in range(2):
                    po, pm = ps[2 + g], ps[6 + g]
                    act(osq.t[:, :], po.t[:, :], AF.Square, [po.s], [osq.s])
                    mm(pm.t[:, :], onesdiv.t[:, :], osq.t[:, :], True, True, [onesdiv.s, osq.s], [pm.s])
                    act(l1.t[:, :], pm.t[:, :], AF.Ln, [pm.s, epsb.s], [l1.s], bias=epsb.t[:, 0:1])
                    act(l1.t[:, :], l1.t[:, :], AF.Exp, [l1.s], [l1.s], scale=-0.5)
                    stt(l2.t[:, :], po.t[:, :], gains.t[:, G_GN:G_GN + 1], l1.t[:, :], ALU.mult, ALU.mult,
                        [po.s, gains.s, l1.s], [l2.s])
                    tt(POOL, ogT.t[:, g * 4:(g + 1) * 4, c0:c0 + P], l2.t[:, :].rearrange("p (h t) -> p h t", t=P),
                       gateT.t[:, g * 4:(g + 1) * 4, c0:c0 + P], ALU.mult, [l2.s, gateT.s], [ogT.s])

            if _stop <= 6:
                break
            for c in range(2):
                wo = wload(wbufs, CH_O + c)
                for sub in range(NSUB):
                    pb = ps[sub]
                    for kc in range(8):
                        mm(pb.t[:, :], ogT.t[:, kc, sub * P:(sub + 1) * P], wo.t[:, kc * 512:(kc + 1) * 512],
                           kc == 0, kc == 7, [wo.s, ogT.s], [pb.s], last=(kc == 7))
                    hv = hs.t[:, sub, c * 512:(c + 1) * 512]
                    tt(DVE, hv, hv, pb.t[:, :], ALU.add, [pb.s, hs.s[sub]], [hs.s[sub]])

            if _stop <= 7:
                break
            mlp(hs, xnT, actT, act_slots, nscope, wbufs, ef, CH_UP0, CH_DN0, G_MLP0, [4, 5], [0, 1, 2, 3])

            if _stop <= 8:
                break
            for sub in range(NSUB):
                ev = dma(POOL, h1_d.ap()[t0 + sub * P: t0 + (sub + 1) * P, :], hs.t[:, sub, :], [hs.s[sub]], [h1_slots[ti]])
                if upto == "A":
                    out_evs.append(ev)
            if _stop <= 9:
                break
            xn1T = ogT
            for sub in range(NSUB):
                rmsnorm_T(nscope, hs, sub, [(xnT.t, [xnT.s], G_KVIN), (xn1T.t, [xn1T.s], G_MLA)])
            if _stop <= 10:
                break
            for sub in range(NSUB):
                pk = ps[sub % 2]
                pq = ps[2 + (sub % 2)]
                for kc in range(8):
                    mm(pk.t[:, 0:384], xnT.t[:, kc, sub * P:(sub + 1) * P], wsm.t[:, kc, 0:384], kc == 0, kc == 7,
                       [wsm.s, xnT.s], [pk.s], last=(kc == 7))
                for kc in range(8):
                    mm(pq.t[:, 0:256], xn1T.t[:, kc, sub * P:(sub + 1) * P], wsm.t[:, kc, 384:640], kc == 0, kc == 7,
                       [wsm.s, xn1T.s], [pq.s], last=(kc == 7))
                if _stop <= 11:
                    break
                act(junk.t[:, 0:256], pk.t[:, 0:256], AF.Square, [pk.s], [junk.s, ss2.s], accum=ss2.t[:, 0:1])
                act(junk.t[:, 256:512], pq.t[:, 0:256], AF.Square, [pq.s], [junk.s, ss2.s], accum=ss2.t[:, 1:2])
                act(ss2.t[:, 2:4], ss2.t[:, 0:2], AF.Ln, [ss2.s, epsb.s], [ss2.s], scale=1.0 / 256, bias=epsb.t[:, 0:1])
                act(ss2.t[:, 4:6], ss2.t[:, 2:4], AF.Exp, [ss2.s], [ss2.s], scale=-0.5)
                ts(DVE, ckn.t[:, :], pk.t[:, 0:256], ss2.t[:, 4:5], None, ALU.mult, None, [pk.s, ss2.s], [ckn.s])
                ts(DVE, cqn.t[:, :], pq.t[:, 0:256], ss2.t[:, 5:6], None, ALU.mult, None, [pq.s, ss2.s], [cqn.s])
                if _stop <= 12:
                    break
                tt(DVE, rt1.t[:, :], pk.t[:, 256:320], costm.t[:, sub, :], ALU.mult, [pk.s, costm.s], [rt1.s])
                tt(DVE, rt2.t[:, :], pk.t[:, 320:384], sintm.t[:, sub, :], ALU.mult, [pk.s, sintm.s], [rt2.s])
                tt(POOL, krb.t[:, 0:64], rt1.t[:, :], rt2.t[:, :], ALU.add, [rt1.s, rt2.s], [krb.s])
                tt(POOL, krb.t[:, 64:128], rt1.t[:, :], rt2.t[:, :], ALU.add, [rt1.s, rt2.s], [krb.s])
                if _stop <= 13:
                    break
                bi = 4 + (sub % 2)
                ptv = psb[bi]
                for cc in range(2):
                    tp(ptv[:, cc * P:(cc + 1) * P], ckn.t[:, cc * P:(cc + 1) * P], [ckn.s], [ps[bi].s], last=False)
                for cc in range(2):
                    tp(ptv[:, (2 + cc) * P:(3 + cc) * P], cqn.t[:, cc * P:(cc + 1) * P], [cqn.s], [ps[bi].s], last=False)
                tp(ptv[:, 4 * P:5 * P], krb.t[:, :], [krb.s], [ps[bi].s], last=True)
                if _stop <= 14:
                    break
                tsl = slice(t0 + sub * P, t0 + (sub + 1) * P)
                for cc in range(2):
                    ts(DVE, ckvT.t[:, cc, tsl], ptv[:, cc * P:(cc + 1) * P], gains.t[:, G_KVN + cc:G_KVN + cc + 1], None,
                       ALU.mult, None, [ps[bi].s, gains.s], [ckvT.s])
                    ts(DVE, cqT.t[:, cc, tsl], ptv[:, (2 + cc) * P:(3 + cc) * P], gains.t[:, G_QN + cc:G_QN + cc + 1], None,
                       ALU.mult, None, [ps[bi].s, gains.s], [cqT.s])
                if not _os.environ.get('DBG_NOKR'):
                    cp(DVE, kropeT.t[:, tsl], ptv[:, 4 * P:5 * P], [ps[bi].s], [kropeT.s])
            if _stop <= 15:
                break
        if "c" in dbgset:
            for (nm, src_) in (("dbg_ckvT", ckvT), ("dbg_cqT", cqT)):
                dd = dram(nm, [P, 2, S], BF16, "ExternalOutput")
                out_evs.append(dma(POOL, dd.ap(), src_.t[:, :, :], [src_.s], []))
            dd = dram("dbg_kropeT", [64, S], BF16, "ExternalOutput")
            out_evs.append(dma(POOL, dd.ap(), kropeT.t[0:64, :], [kropeT.s], []))
        pg.flush(final_waits=out_evs if upto == "A" else ())
    if upto == "A":
        return nc, es

    with ExitStack() as sbs:
        cosfm = Buf(sb("cosfm", [P, S], F32, sbs))
        sinfm = Buf(sb("sinfm", [P, S], F32, sbs))
        whb = [Buf(sb("whb%d" % i, [P, 2, 640], BF16, sbs)) for i in range(2)]
        KhT = [Buf(sb("KhT%d" % i, [P, S], BF16, sbs)) for i in range(2)]
        QnT = [Buf(sb("QnT%d" % i, [P, S], BF16, sbs)) for i in range(2)]
        QrT = [Buf(sb("QrT%d" % i, [P, S], BF16, sbs)) for i in range(2)]
        Vh = [Buf(sb("Vh%d" % i, [P, NTT, P], BF16, sbs)) for i in range(2)]
        pT = [Buf(sb("pT%d" % i, [P, 512], BF16, sbs)) for i in range(6)]
        q1 = [Buf(sb("q1_%d" % i, [P, 512], F32, sbs)) for i in range(2)]
        q2 = [Buf(sb("q2_%d" % i, [P, 512], F32, sbs)) for i in range(2)]
        accD = [Buf(sb("accD%d" % i, [P, 512], F32, sbs)) for i in range(2)]
        accP = [Buf(sb("accP%d" % i, [P, 512], F32, sbs)) for i in range(2)]
        rc = [Buf(sb("rc%d" % i, [P, 512], F32, sbs)) for i in range(2)]
        ost = [Buf(sb("ost%d" % i, [P, 512], BF16, sbs)) for i in range(2)]
        dma(SP, cosfm.t[:, :], cosfm_d.ap(), [], [cosfm.s])
        dma(SP, sinfm.t[:, :], sinfm_d.ap(), [], [sinfm.s])
        evac_rr = [0]

        def evac(dst_ap, src_ap, reads, writes):
            i = evac_rr[0]
            evac_rr[0] += 1
            if i % 2 == 0:
                act(dst_ap, src_ap, AF.Copy, reads, writes)
            else:
                cp(DVE, dst_ap, src_ap, reads, writes)

        qbi = 0
        ptc = 0
        for h in range(NH):
            r = h % 2
            wh_ = whb[r]
            dma(POOL, wh_.t[:, :, :], wh_d.ap()[h], [], [wh_.s])
            for tb in range(NTB):
                tsl = slice(tb * 512, (tb + 1) * 512)
                for (c0_, c1_, src_, dst_) in ((0, 128, ckvT, KhT[r]), (256, 384, cqT, QnT[r])):
                    pb = ps[7]
                    for cc in range(2):
                        mm(pb.t[:, :], wh_.t[:, cc, c0_:c1_], src_.t[:, cc, tsl], cc == 0, cc == 1,
                           [wh_.s, src_.s], [pb.s], last=(cc == 1))
                    evac(dst_.t[:, tsl], pb.t[:, :], [pb.s], [dst_.s])
                pa, pbb = ps[7], ps[6]
                for (pb, c0_) in ((pa, 384), (pbb, 512)):
                    for cc in range(2):
                        mm(pb.t[:, :], wh_.t[:, cc, c0_:c0_ + 128], cqT.t[:, cc, tsl], cc == 0, cc == 1,
                           [wh_.s, cqT.s], [pb.s], last=(cc == 1))
                rr = tb % 2
                tt(DVE, q1[rr].t[:, :], pa.t[:, :], cosfm.t[:, tsl], ALU.mult, [pa.s, cosfm.s], [q1[rr].s])
                tt(DVE, q2[rr].t[:, :], pbb.t[:, :], sinfm.t[:, tsl], ALU.mult, [pbb.s, sinfm.s], [q2[rr].s])
                tt(POOL, QrT[r].t[:, tsl], q1[rr].t[:, :], q2[rr].t[:, :], ALU.add, [q1[rr].s, q2[rr].s], [QrT[r].s])
            for g4 in range(NTT // 4):
                pv = ps[7 - (g4 % 2)]
                for i4 in range(4):
                    tt_ = g4 * 4 + i4
                    for cc in range(2):
                        mm(pv.t[:, i4 * P:(i4 + 1) * P], ckvT.t[:, cc, tt_ * P:(tt_ + 1) * P], wh_.t[:, cc, 128:256],
                           cc == 0, cc == 1, [wh_.s, ckvT.s], [pv.s], last=(cc == 1 and i4 == 3))
                evac(Vh[r].t[:, g4 * 4:(g4 + 1) * 4, :], pv.t[:, :].rearrange("p (a b) -> p a b", b=P), [pv.s], [Vh[r].s])

            for qb in range(NTB):
                po = ps[4 + (qbi % 2)]
                pz = ps[6]
                aD, aP = accD[qbi % 2], accP[qbi % 2]
                rr = qbi % 2
                qbi += 1
                nkt = 4 * (qb + 1)
                tiles = []
                for kt in range(nkt):
                    q0 = 0 if kt < 4 * qb else (kt - 4 * qb) * P
                    tiles.append((kt, q0))
                mset(POOL, aP.t[:, :], 0.0, [aP.s])
                ptb = {}

                def qk_pair(pi):
                    info = []
                    for j in range(2):
                        idx = 2 * pi + j
                        kt, q0 = tiles[idx]
                        pss = ps[(2 * pi + j) % 4]
                        n = 512 - q0
                        qsl = slice(qb * 512 + q0, (qb + 1) * 512)
                        info.append((kt, pss, n, qsl, j))
                    for (kt, pss, n, qsl, j) in info:
                        mm(pss.t[:, 0:n], KhT[r].t[:, kt * P:(kt + 1) * P], QnT[r].t[:, qsl], True, False,
                           [KhT[r].s, QnT[r].s], [pss.s], last=False)
                    for (kt, pss, n, qsl, j) in info:
                        mm(pss.t[:, 0:n], kropeT.t[j * 64:(j + 1) * 64, kt * P:(kt + 1) * P],
                           QrT[r].t[j * 64:(j + 1) * 64, qsl], False, True,
                           [kropeT.s, QrT[r].s], [pss.s], last=(j == 1))

                def pv_(idx):
                    nonlocal ptc
                    kt, q0 = tiles[idx]
                    pss = ps[idx % 4]
                    pt_ = pT[ptc % len(pT)]
                    ptc += 1
                    n = 512 - q0
                    act(pt_.t[:, 0:n], pss.t[:, 0:n], AF.Exp, [pss.s], [pt_.s], scale=ATT_SCALE)
                    if kt >= 4 * qb:
                        tt(POOL, pt_.t[:, 0:P], pt_.t[:, 0:P], trimask.t[:, :], ALU.mult, [pt_.s, trimask.s], [pt_.s])
                    lastk = (idx == nkt - 1)
                    mm(po.t[:, q0:512], Vh[r].t[:, kt, :], pt_.t[:, 0:n], idx == 0, lastk, [Vh[r].s, pt_.s], [po.s],
                       last=True, skip=True)
                    if idx == 0:
                        cp(DVE, aD.t[:, :], pt_.t[:, :], [pt_.s], [aD.s])
                    elif idx % 3 == 2:
                        tt(POOL, aP.t[:, q0:512], aP.t[:, q0:512], pt_.t[:, 0:n], ALU.add, [aP.s, pt_.s], [aP.s])
                    else:
                        tt(DVE, aD.t[:, q0:512], aD.t[:, q0:512], pt_.t[:, 0:n], ALU.add, [aD.s, pt_.s], [aD.s])

                npair = nkt // 2
                qk_pair(0)
                for pi in range(npair):
                    if pi + 1 < npair:
                        qk_pair(pi + 1)
                    pv_(2 * pi)
                    pv_(2 * pi + 1)
                tt(DVE, aD.t[:, :], aD.t[:, :], aP.t[:, :], ALU.add, [aD.s, aP.s], [aD.s])
                mm(pz.t[:, :], ones32.t[:, :], aD.t[:, :], True, True, [ones32.s, aD.s], [pz.s])
                act(rc[rr].t[:, :], pz.t[:, :], AF.Ln, [pz.s], [rc[rr].s])
                act(rc[rr].t[:, :], rc[rr].t[:, :], AF.Exp, [rc[rr].s], [rc[rr].s], scale=-1.0)
                tt(DVE, ost[rr].t[:, :], po.t[:, :], rc[rr].t[:, :], ALU.mult, [po.s, rc[rr].s], [ost[rr].s])
                ev = dma(POOL, o_d.ap()[h, :, qb * 512:(qb + 1) * 512], ost[rr].t[:, :], [ost[rr].s], [o_slots[h][qb]])
                if upto == "B":
                    out_evs.append(ev)
        pg.flush(final_waits=out_evs if upto == "B" else ())
    if upto == "B":
        return nc, es

    with ExitStack() as sc_:
        hs = HS(sb("hsc", [P, NSUB, D], F32, sc_), "hsc")
        oT = [Buf(sb("oTc%d" % i, [P, NH, T], BF16, sc_)) for i in range(2)]
        xnT = Buf(sb("xnTc", [P, 8, T], BF16, sc_))
        junk = Buf(sb("junkc", [P, D], BF16, sc_))
        ss = Buf(sb("ssc", [P, 4], F32, sc_))
        xh = Buf(sb("xhc", [P, D], BF16, sc_))
        wbufs = [Buf(sb("wbufc%d" % i, [P, 4096], BF16, sc_)) for i in range(5)]
        actTb = Buf(sb("actTc", [P, 32, T], BF16, sc_))
        sqb = [Buf(sb("sqbc%d" % i, [P, T], F32, sc_)) for i in range(2)]
        fnb = Buf(sb("fnb", [P, D], F32, sc_))
        outb = [Buf(sb("outb%d" % i, [P, D], F32, sc_)) for i in range(2)]
        nscope = (junk, ss, xh, 7)
        dma(SP, fnb.t[:, :], fnorm_d.ap(), [], [fnb.s])
        for ti in range(NT):
            ot = oT[ti % 2]
            t0 = ti * T
            for sub in range(NSUB):
                dma(POOL, hs.t[:, sub, :], h1_d.ap()[t0 + sub * P: t0 + (sub + 1) * P, :], [h1_slots[ti]], [hs.s[sub]])
            for h in range(NH):
                dma(POOL, ot.t[:, h, :], o_d.ap()[h, :, t0:t0 + T], [o_slots[h][ti]], [ot.s])
            for j in range(2):
                for i in range(2):
                    wb = wload(wbufs, CH_WO1 + j * 2 + i)
                    for sub in range(NSUB):
                        pb = ps[sub]
                        for kc in range(8):
                            mm(pb.t[:, :], ot.t[:, i * 8 + kc, sub * P:(sub + 1) * P], wb.t[:, kc * 512:(kc + 1) * 512],
                               (i == 0 and kc == 0), (i == 1 and kc == 7), [wb.s, ot.s], [pb.s], last=(kc == 7))
                for sub in range(NSUB):
                    pb = ps[sub]
                    hv = hs.t[:, sub, j * 512:(j + 1) * 512]
                    tt(DVE, hv, hv, pb.t[:, :], ALU.add, [pb.s, hs.s[sub]], [hs.s[sub]])
            mlp(hs, xnT, actTb.t[:, :, :], [actTb.s], nscope, wbufs, sqb, CH_UP1, CH_DN1, G_MLP1, [4, 5], [0, 1, 2, 3])
            for sub in range(NSUB):
                ob = outb[sub % 2]
                hsl = hs.t[:, sub, :]
                act(junk.t[:, :], hsl, AF.Square, [hs.s[sub]], [junk.s, ss.s], accum=ss.t[:, 0:1])
                act(ss.t[:, 1:2], ss.t[:, 0:1], AF.Ln, [ss.s, epsb.s], [ss.s], scale=1.0 / D, bias=epsb.t[:, 0:1])
                act(ss.t[:, 2:3], ss.t[:, 1:2], AF.Exp, [ss.s], [ss.s], scale=-0.5)
                stt(ob.t[:, :], hsl, ss.t[:, 2:3], fnb.t[:, :], ALU.mult, ALU.mult, [hs.s[sub], ss.s, fnb.s], [ob.s])
                ev = dma(POOL, out_d.ap()[t0 + sub * P: t0 + (sub + 1) * P, :], ob.t[:, :], [ob.s], [])
                out_evs.append(ev)
        pg.flush(final_waits=out_evs)
    return nc, es


_CACHE = {}


def kernel(**inputs):
    S = inputs["x"].shape[1]
    ncores = inputs["x"].shape[0]
    inp = {k: np.asarray(v) for k, v in inputs.items()}
    packed = _pack_weights(inp)
    consts = _consts(S)
    nc, es = build(S=S, upto="C", debug=False)
    in_maps = []
    for b in range(ncores):
        m = {"x": np.ascontiguousarray(inp["x"][b], dtype=np.float32)}
        m.update(packed)
        m.update(consts)
        in_maps.append(m)
    res = run_bass_kernel_spmd(nc, in_maps, core_ids=list(range(ncores)))
    es.close()
    out = np.stack([np.asarray(r["out"]) for r in res.results], axis=0).astype(np.float32)
    return out
```

```python
import numpy as np
import ml_dtypes
from contextlib import ExitStack
import concourse.bass as bass
import concourse.mybir as mybir
from concourse.bass_utils import run_bass_kernel_spmd

F32 = mybir.dt.float32
BF16 = mybir.dt.bfloat16
AF = mybir.ActivationFunctionType
ALU = mybir.AluOpType

P = 128
D = 1024
DFF = 4096
T = 512
NSUB = 4
EPS = 1e-6
NH = 16
ATT_SCALE = float((128 + 64) ** -0.5)
SEM_LIMIT = 12000


class Ev:
    __slots__ = ("sem", "val", "eng", "key")

    def __init__(self, eng):
        self.sem = None
        self.val = None
        self.eng = eng
        self.key = None


class Slot:
    __slots__ = ("name", "writers", "readers")

    def __init__(self, name=""):
        self.name = name
        self.writers = []
        self.readers = []


class Eng:
    def __init__(self, prog, name):
        self.prog = prog
        self.name = name
        self.ops = []
        self.pending = []
        self.sem = None
        self.semkey = None
        self.cnt = 0
        self.waited = {}
        self.dma_sems = []
        self.dma_cnt = []
        self.dma_rr = 0

    def _new_sem(self):
        self.sem, self.semkey = self.prog.new_sem(self.name)
        self.cnt = 0

    def op(self, fn, reads=(), writes=(), kind="inc"):
        ev = Ev(self if kind != "dma" else None)
        waits = []
        for sl in reads:
            waits.extend(sl.writers)
        same_ok = (self.name == "pe")
        for sl in writes:
            for e in sl.readers:
                if e.eng is not self or not same_ok:
                    waits.append(e)
            for e in sl.writers:
                if e.eng is not self or not same_ok:
                    waits.append(e)
        for sl in reads:
            sl.readers.append(ev)
        for sl in writes:
            if sl.readers:
                sl.writers = [ev]
                sl.readers = []
            else:
                sl.writers.append(ev)
        if kind == "inc":
            if self.sem is None or self.cnt >= SEM_LIMIT:
                self._new_sem()
            self.cnt += 1
            ev.sem, ev.key, ev.val = self.sem, self.semkey, self.cnt
            for pe in self.pending:
                pe.sem, pe.key, pe.val = ev.sem, ev.key, ev.val
            self.pending = []
        elif kind == "dma":
            if not self.dma_sems:
                for i in range(self.prog.nds):
                    s, k = self.prog.new_sem(self.name + "_dma%d" % i)
                    self.dma_sems.append((s, k))
                    self.dma_cnt.append(0)
            j = self.dma_rr % len(self.dma_sems)
            self.dma_rr += 1
            s, k = self.dma_sems[j]
            if self.dma_cnt[j] > 0:
                pw = Ev(None)
                pw.sem, pw.key, pw.val = s, k, self.dma_cnt[j]
                waits.append(pw)
            self.dma_cnt[j] += 16
            ev.sem, ev.key, ev.val = s, k, self.dma_cnt[j]
        else:
            self.pending.append(ev)
        self.ops.append((waits, fn, ev, kind))
        return ev

    def emit(self, e):
        for (waits, fn, ev, kind) in self.ops:
            for w in waits:
                assert w.sem is not None, "unresolved event"
                if self.waited.get(w.key, 0) < w.val:
                    e.wait_ge(w.sem, w.val)
                    self.waited[w.key] = w.val
            ins = fn(e)
            if kind == "inc":
                ins.then_inc(ev.sem, 1)
            elif kind == "dma":
                ins.then_inc(ev.sem, 16)
        self.ops = []


class Prog:
    def __init__(self, nc, es, nds=12):
        self.nc = nc
        self.es = es
        self.nds = nds
        self.nsem = 0
        self.pe = Eng(self, "pe")
        self.act = Eng(self, "act")
        self.dve = Eng(self, "dve")
        self.pool = Eng(self, "pool")
        self.sp = Eng(self, "sp")

    def new_sem(self, name):
        self.nsem += 1
        s = self.es.enter_context(self.nc.semaphore("s%d_%s" % (self.nsem, name)))
        return s, self.nsem

    def flush(self, final_waits=()):
        nc = self.nc
        for en in (self.pe, self.act, self.dve, self.pool, self.sp):
            assert not en.pending, "pending events on %s" % en.name
        with nc.Block() as block:
            if self.pe.ops:
                @block.tensor
                def _(e):
                    self.pe.emit(e)
            if self.act.ops:
                @block.scalar
                def _(e):
                    self.act.emit(e)
            if self.dve.ops:
                @block.vector
                def _(e):
                    self.dve.emit(e)
            if True:
                @block.gpsimd
                def _(e):
                    self.pool.emit(e)
                    for w in final_waits:
                        e.wait_ge(w.sem, w.val)
                    for en in (self.sp, self.pool):
                        for (s, k), c in zip(en.dma_sems, en.dma_cnt):
                            if c > 0:
                                e.wait_ge(s, c)
            if self.sp.ops:
                @block.sync
                def _(e):
                    self.sp.emit(e)


class Buf:
    __slots__ = ("t", "s")

    def __init__(self, t, name=""):
        self.t = t
        self.s = Slot(name)


def _rope_tables(S):
    half = 32
    inv = (np.float32(10000.0) ** (-(np.arange(half, dtype=np.float32) / np.float32(half)))).astype(np.float32)
    pos = np.arange(S, dtype=np.float32)
    ang = (pos[:, None] * inv[None, :]).astype(np.float32)
    cos = np.cos(ang).astype(np.float32)
    sin = np.sin(ang).astype(np.float32)
    cos2 = np.concatenate([cos, cos], axis=1)
    sins = np.concatenate([-sin, sin], axis=1)
    return cos2, sins


def _consts(S):
    c = {}
    c["ident"] = np.eye(P, dtype=np.float32).astype(ml_dtypes.bfloat16)
    s = np.arange(P)[:, None]
    t = np.arange(P)[None, :]
    c["bdmask"] = (((s // 64) == (t // 64)) & (s <= t)).astype(np.float32)
    c["trimask"] = (s <= t).astype(np.float32).astype(ml_dtypes.bfloat16)
    c["ones"] = np.ones((P, P), dtype=np.float32).astype(ml_dtypes.bfloat16)
    c["onesdiv"] = np.full((P, P), 1.0 / 128.0, dtype=np.float32).astype(ml_dtypes.bfloat16)
    cm = np.ones((P, T), dtype=np.float32)
    cm[:, ::64] = 0.0
    c["cmask"] = cm
    cos2, sins = _rope_tables(S)
    c["cos_fm"] = np.ascontiguousarray(np.concatenate([cos2.T, cos2.T], axis=0))
    c["sin_fm"] = np.ascontiguousarray(np.concatenate([sins.T, sins.T], axis=0))
    c["ones32"] = np.ones((P, P), dtype=np.float32)
    kk_ = np.arange(P)[:, None]
    qq_ = np.arange(P)[None, :]
    c["negmask"] = np.where(kk_ > qq_, -30000.0, 0.0).astype(np.float32).astype(ml_dtypes.bfloat16)
    c["cos_tm"] = np.ascontiguousarray(cos2.reshape(S // P, P, 64).transpose(1, 0, 2))
    c["sin_tm"] = np.ascontiguousarray(sins.reshape(S // P, P, 64).transpose(1, 0, 2))
    return c


def _chunkify(W):
    R, C = W.shape
    out = []
    for j in range(C // 512):
        for i in range(R // 1024):
            blk = W[i * 1024:(i + 1) * 1024, j * 512:(j + 1) * 512]
            out.append(blk.reshape(8, P, 512).transpose(1, 0, 2).reshape(P, 4096))
    return out


def _fm(v):
    v = np.asarray(v, dtype=np.float32).reshape(-1, P)
    return v.T


CH_F, CH_Q, CH_I, CH_G, CH_O = 0, 2, 4, 6, 8
CH_UP0, CH_DN0 = 10, 18
CH_WO1 = 26
CH_UP1, CH_DN1 = 30, 38
NCH = 46
G_HGRN, G_MLP0, G_MLP1, G_KVIN, G_MLA = 0, 8, 16, 24, 32
G_KVN, G_QN, G_GN, G_L0, G_L1 = 40, 42, 44, 45, 53
NG = 61


def _pack_weights(inp):
    chunks = []
    for nm in ("hgrn_w_f", "hgrn_w_q", "hgrn_w_i", "hgrn_w_g", "hgrn_w_o"):
        chunks += _chunkify(inp[nm][0])
    chunks += _chunkify(inp["mlp_w_up"][0])
    chunks += _chunkify(inp["mlp_w_down"][0])
    chunks += _chunkify(inp["mla_w_o"][0])
    chunks += _chunkify(inp["mlp_w_up"][1])
    chunks += _chunkify(inp["mlp_w_down"][1])
    assert len(chunks) == NCH
    wbig = np.ascontiguousarray(np.stack(chunks, axis=0), dtype=np.float32)

    wd = inp["kv_w_dkv"]
    wdkv = np.concatenate([wd, wd[:, 288:320], wd[:, 256:288]], axis=1)
    wdkv = wdkv.reshape(8, P, 384).transpose(1, 0, 2)
    wdq = inp["mla_w_dq"][0].reshape(8, P, 256).transpose(1, 0, 2)
    wsm = np.ascontiguousarray(np.concatenate([wdkv, wdq], axis=2), dtype=np.float32)

    uk, uv, uq = inp["kv_w_uk"], inp["kv_w_uv"], inp["mla_w_uq"][0]
    wh = np.empty((NH, 256, 640), dtype=np.float32)
    for h in range(NH):
        wh[h, :, 0:128] = uk[:, h * 128:(h + 1) * 128]
        wh[h, :, 128:256] = uv[:, h * 128:(h + 1) * 128]
        wh[h, :, 256:384] = uq[:, h * 192:h * 192 + 128]
        rope = uq[:, h * 192 + 128:h * 192 + 192]
        swap = np.concatenate([uq[:, h * 192 + 160:h * 192 + 192], uq[:, h * 192 + 128:h * 192 + 160]], axis=1)
        wh[h, :, 384:448] = rope
        wh[h, :, 448:512] = rope
        wh[h, :, 512:576] = swap
        wh[h, :, 576:640] = swap
    wh = np.ascontiguousarray(wh.reshape(NH, 2, P, 640).transpose(0, 2, 1, 3))

    g = np.empty((P, NG), dtype=np.float32)
    g[:, G_HGRN:G_HGRN + 8] = _fm(inp["hgrn_norm"][0])
    g[:, G_MLP0:G_MLP0 + 8] = _fm(inp["mlp_norm"][0])
    g[:, G_MLP1:G_MLP1 + 8] = _fm(inp["mlp_norm"][1])
    g[:, G_KVIN:G_KVIN + 8] = _fm(inp["kv_in_norm"])
    g[:, G_MLA:G_MLA + 8] = _fm(inp["mla_norm"][0])
    g[:, G_KVN:G_KVN + 2] = _fm(inp["kv_norm"])
    g[:, G_QN:G_QN + 2] = _fm(inp["mla_q_norm"][0])
    g[:, G_GN:G_GN + 1] = _fm(inp["hgrn_g_norm"][0])
    g[:, G_L0:G_L0 + 8] = _fm(inp["hgrn_lb_logits"][0])
    g[:, G_L1:G_L1 + 8] = _fm(inp["hgrn_lb_logits"][1])
    fn = np.ascontiguousarray(np.broadcast_to(inp["final_norm"].astype(np.float32)[None, :], (P, D)))
    return {"wbig": wbig, "wsm": wsm, "wh": wh, "gains": g, "fnorm": fn}


def build(S=4096, upto="C", debug=False):
    import os as _os
    nc = bass.Bass("TRN2", target_bir_lowering=False)
    NT = S // T
    NTT = S // P
    NTB = S // 512
    es = ExitStack()
    pg = Prog(nc, es)
    PE, ACT, DVE, POOL, SP = pg.pe, pg.act, pg.dve, pg.pool, pg.sp
    dbgset = set(debug.split(",")) if isinstance(debug, str) else set()

    def mm(out, lhsT, rhs, start, stop, reads, writes, last=True, skip=False):
        return PE.op(lambda e: e.matmul(out, lhsT, rhs, start=start, stop=stop, skip_group_check=skip),
                     reads=reads, writes=writes, kind=("inc" if last else "noinc"))

    def tp(out, in_, reads, writes, last=True):
        idn = ident.t[0:in_.shape[0], 0:in_.shape[0]]
        return PE.op(lambda e: e.transpose(out=out, in_=in_, identity=idn),
                     reads=list(reads) + [ident.s], writes=writes, kind=("inc" if last else "noinc"))

    def act(out, in_, func, reads, writes, scale=None, bias=None, accum=None):
        kw = {}
        if scale is not None:
            kw["scale"] = scale
        if bias is not None:
            kw["bias"] = bias
        if accum is not None:
            kw["accum_out"] = accum
        return ACT.op(lambda e: e.activation(out=out, in_=in_, func=func, **kw), reads=reads, writes=writes)

    def tt(E, out, in0, in1, op, reads, writes):
        return E.op(lambda e: e.tensor_tensor(out=out, in0=in0, in1=in1, op=op), reads=reads, writes=writes)

    def ts(E, out, in0, s1, s2, op0, op1, reads, writes):
        if s2 is None:
            return E.op(lambda e: e.tensor_scalar(out=out, in0=in0, scalar1=s1, scalar2=None, op0=op0),
                        reads=reads, writes=writes)
        return E.op(lambda e: e.tensor_scalar(out=out, in0=in0, scalar1=s1, scalar2=s2, op0=op0, op1=op1),
                    reads=reads, writes=writes)

    def stt(out, in0, scalar, in1, op0, op1, reads, writes):
        return DVE.op(lambda e: e.scalar_tensor_tensor(out=out, in0=in0, scalar=scalar, in1=in1, op0=op0, op1=op1),
                      reads=reads, writes=writes)

    def cp(E, out, in_, reads, writes):
        return E.op(lambda e: e.tensor_copy(out=out, in_=in_), reads=reads, writes=writes)

    def mset(E, ap, val, writes):
        return E.op(lambda e: e.memset(ap, val), writes=writes)

    def dma(Q, out, in_, reads, writes):
        return Q.op(lambda e: e.dma_start(out=out, in_=in_), reads=reads, writes=writes, kind="dma")

    def dram(name, shape, dt, kind):
        return nc.dram_tensor(name, list(shape), dt, kind=kind)

    x_d = dram("x", [S, D], F32, "ExternalInput")
    wbig_d = dram("wbig", [NCH, P, 4096], F32, "ExternalInput")
    wsm_d = dram("wsm", [P, 8, 640], F32, "ExternalInput")
    wh_d = dram("wh", [NH, P, 2, 640], F32, "ExternalInput")
    gains_d = dram("gains", [P, NG], F32, "ExternalInput")
    fnorm_d = dram("fnorm", [P, D], F32, "ExternalInput")
    ident_d = dram("ident", [P, P], BF16, "ExternalInput")
    bdmask_d = dram("bdmask", [P, P], F32, "ExternalInput")
    trimask_d = dram("trimask", [P, P], BF16, "ExternalInput")
    ones_d = dram("ones", [P, P], BF16, "ExternalInput")
    onesdiv_d = dram("onesdiv", [P, P], BF16, "ExternalInput")
    cmask_d = dram("cmask", [P, T], F32, "ExternalInput")
    cosfm_d = dram("cos_fm", [P, S], F32, "ExternalInput")
    sinfm_d = dram("sin_fm", [P, S], F32, "ExternalInput")
    ones32_d = dram("ones32", [P, P], F32, "ExternalInput")
    negmask_d = dram("negmask", [P, P], BF16, "ExternalInput")
    costm_d = dram("cos_tm", [P, NTT, 64], F32, "ExternalInput")
    sintm_d = dram("sin_tm", [P, NTT, 64], F32, "ExternalInput")
    out_d = dram("out", [S, D], F32, "ExternalOutput")
    wb_d = dram("wb_scr", [NCH, P, 4096], BF16, "Internal")
    h1_d = dram("h1_scr", [S, D], F32, "ExternalOutput" if "h1" in dbgset else "Internal")
    o_d = dram("o_scr", [NH, P, S], BF16, "ExternalOutput" if "o" in dbgset else "Internal")
    wb_slots = [Slot("wb%d" % i) for i in range(NCH)]
    h1_slots = [Slot("h1_%d" % i) for i in range(NT)]
    o_slots = [[Slot("o_%d_%d" % (h, i)) for i in range(NTB)] for h in range(NH)]
    out_evs = []

    def sb(name, shape, dt, scope):
        return scope.enter_context(nc.sbuf_tensor("sb_" + name, list(shape), dt))

    ps = [Buf(es.enter_context(nc.psum_tensor("ps%d" % i, [P, 512], F32)), "ps%d" % i) for i in range(8)]
    psb = [p_.t[:, :].bitcast(BF16) for p_ in ps]

    ident = Buf(sb("ident", [P, P], BF16, es))
    bdmask = Buf(sb("bdmask", [P, P], F32, es))
    trimask = Buf(sb("trimask", [P, P], BF16, es))
    ones = Buf(sb("ones", [P, P], BF16, es))
    onesdiv = Buf(sb("onesdiv", [P, P], BF16, es))
    gains = Buf(sb("gains", [P, NG], F32, es))
    lbt = Buf(sb("lbt", [P, 32], F32, es))
    epsb = Buf(sb("epsb", [P, 1], F32, es))
    ckvT = Buf(sb("ckvT", [P, 2, S], BF16, es))
    cqT = Buf(sb("cqT", [P, 2, S], BF16, es))
    kropeT = Buf(sb("kropeT", [P, S], BF16, es))
    ones32 = Buf(sb("ones32", [P, P], F32, es))
    negmask = Buf(sb("negmask", [P, P], BF16, es))

    dma(SP, ident.t[:, :], ident_d.ap(), [], [ident.s])
    dma(SP, bdmask.t[:, :], bdmask_d.ap(), [], [bdmask.s])
    dma(SP, trimask.t[:, :], trimask_d.ap(), [], [trimask.s])
    dma(SP, ones.t[:, :], ones_d.ap(), [], [ones.s])
    dma(SP, onesdiv.t[:, :], onesdiv_d.ap(), [], [onesdiv.s])
    dma(SP, gains.t[:, :], gains_d.ap(), [], [gains.s])
    dma(SP, ones32.t[:, :], ones32_d.ap(), [], [ones32.s])
    dma(SP, negmask.t[:, :], negmask_d.ap(), [], [negmask.s])
    mset(POOL, epsb.t[:, :], EPS, [epsb.s])
    tt(DVE, lbt.t[:, 24:32], gains.t[:, G_L1:G_L1 + 8], gains.t[:, G_L0:G_L0 + 8], ALU.subtract, [gains.s], [lbt.s])
    act(lbt.t[:, 24:32], lbt.t[:, 24:32], AF.Exp, [lbt.s], [lbt.s])
    act(lbt.t[:, 24:32], lbt.t[:, 24:32], AF.Ln, [lbt.s], [lbt.s], bias=1.0)
    act(lbt.t[:, 0:8], lbt.t[:, 24:32], AF.Exp, [lbt.s], [lbt.s], scale=-1.0)
    ts(DVE, lbt.t[:, 8:16], lbt.t[:, 0:8], -1.0, 1.0, ALU.mult, ALU.add, [lbt.s], [lbt.s])

    def cast_chunks(lo, hi):
        for c in range(lo, hi):
            dma(POOL, wb_d.ap()[c], wbig_d.ap()[c], [], [wb_slots[c]])

    if upto == "0":
        cast_chunks(0, NCH)
    if upto == "0":
        pg.flush(final_waits=[w for sl in wb_slots for w in sl.writers])
        return nc, es

    class HS:
        def __init__(self, t, name):
            self.t = t
            self.s = [Slot("%s_%d" % (name, i)) for i in range(NSUB)]

    def rmsnorm_T(nscope, hs, sub, dsts):
        junk, ss, xh, pbi = [v[sub % len(v)] for v in nscope]
        hsl = hs.t[:, sub, :]
        act(junk.t[:, :], hsl, AF.Square, [hs.s[sub]], [junk.s, ss.s], accum=ss.t[:, 0:1])
        act(ss.t[:, 1:2], ss.t[:, 0:1], AF.Ln, [ss.s, epsb.s], [ss.s], scale=1.0 / D, bias=epsb.t[:, 0:1])
        act(ss.t[:, 2:3], ss.t[:, 1:2], AF.Exp, [ss.s], [ss.s], scale=-0.5)
        ts(DVE, xh.t[:, :], hsl, ss.t[:, 2:3], None, ALU.mult, None, [hs.s[sub], ss.s], [xh.s])
        pb = psb[pbi]
        for kc in range(8):
            tp(pb[:, kc * P:(kc + 1) * P], xh.t[:, kc * P:(kc + 1) * P], [xh.s], [ps[pbi].s], last=(kc == 7))
        for (dst3, dslots, gcol) in dsts:
            gb = gains.t[:, gcol:gcol + 8].unsqueeze(2).to_broadcast([P, 8, P])
            tt(DVE, dst3[:, :, sub * P:(sub + 1) * P], pb.rearrange("p (k t) -> p k t", t=P), gb, ALU.mult,
               [ps[pbi].s, gains.s], dslots)

    wrr = [0]

    def wload(wbufs, c):
        b = wbufs[wrr[0] % len(wbufs)]
        wrr[0] += 1
        dma(SP, b.t[:, :], wb_d.ap()[c], [wb_slots[c]], [b.s])
        return b

    def mlp(hs, xnT, actT, act_slots, nscope, wbufs, sqb, ch_up, ch_dn, gcol, pbanks_up, pbanks_dn):
        for sub in range(NSUB):
            rmsnorm_T(nscope, hs, sub, [(xnT.t, [xnT.s], gcol)])
        k = 0
        for j in range(8):
            wb = wload(wbufs, ch_up + j)
            for fl in range(4):
                fb = j * 4 + fl
                pb = ps[pbanks_up[k % len(pbanks_up)]]
                sq = sqb[k % len(sqb)]
                k += 1
                for kc in range(8):
                    mm(pb.t[:, :], wb.t[:, kc * 512 + fl * P: kc * 512 + (fl + 1) * P], xnT.t[:, kc, :],
                       kc == 0, kc == 7, [wb.s, xnT.s], [pb.s], last=(kc == 7))
                act(sq.t[:, :], pb.t[:, :], AF.Square, [pb.s], [sq.s])
                stt(actT[:, fb, :], pb.t[:, :], 0.0, sq.t[:, :], ALU.is_gt, ALU.mult, [pb.s, sq.s], act_slots)
        for hh in range(2):
            for i in range(4):
                wb = wload(wbufs, ch_dn + hh * 4 + i)
                for sub in range(NSUB):
                    pb = ps[pbanks_dn[sub]]
                    for kc in range(8):
                        mm(pb.t[:, :], actT[:, i * 8 + kc, sub * P:(sub + 1) * P], wb.t[:, kc * 512:(kc + 1) * 512],
                           (i == 0 and kc == 0), (i == 3 and kc == 7), [wb.s] + act_slots, [pb.s], last=(kc == 7))
            for sub in range(NSUB):
                pb = ps[pbanks_dn[sub]]
                hv = hs.t[:, sub, hh * 512:(hh + 1) * 512]
                tt(DVE, hv, hv, pb.t[:, :], ALU.add, [pb.s, hs.s[sub]], [hs.s[sub]])

    def sigmoid_chain(pz, buf):
        act(buf.t[:, :], pz.t[:, :], AF.Exp, [pz.s], [buf.s], scale=-1.0)
        act(buf.t[:, :], buf.t[:, :], AF.Ln, [buf.s], [buf.s], bias=1.0)
        act(buf.t[:, :], buf.t[:, :], AF.Exp, [buf.s], [buf.s], scale=-1.0)

    with ExitStack() as sa:
        hs = HS(sb("hs", [P, NSUB, D], F32, sa), "hs")
        xnT = Buf(sb("xnT", [P, 8, T], BF16, sa))
        junk2 = [Buf(sb("junk%d" % i, [P, D], BF16, sa)) for i in range(2)]
        ss_2 = [Buf(sb("ss%d" % i, [P, 4], F32, sa)) for i in range(2)]
        xh2 = [Buf(sb("xh%d" % i, [P, D], BF16, sa)) for i in range(2)]
        junk = junk2[0]
        wbufs = [Buf(sb("wbuf%d" % i, [P, 4096], BF16, sa)) for i in range(3)]
        wsm = Buf(sb("wsm", [P, 8, 640], BF16, sa))
        cmask = Buf(sb("cmask", [P, T], F32, sa))
        costm = Buf(sb("costm", [P, NSUB, 64], F32, sa))
        sintm = Buf(sb("sintm", [P, NSUB, 64], F32, sa))
        ef = [Buf(sb("ef%d" % i, [P, T], F32, sa)) for i in range(2)]
        eq = [Buf(sb("eq%d" % i, [P, T], F32, sa)) for i in range(3)]
        l1 = Buf(sb("l1", [P, T], F32, sa))
        l2 = Buf(sb("l2", [P, T], F32, sa))
        sf = Buf(sb("sf", [P, T], F32, sa))
        kk = Buf(sb("kk", [P, T], F32, sa))
        bb = Buf(sb("bb", [P, T], F32, sa))
        eb = Buf(sb("eb", [P, T], F32, sa))
        enb = Buf(sb("enb", [P, T], F32, sa))
        qs = Buf(sb("qs", [P, T], F32, sa))
        kt32 = Buf(sb("kt32", [P, T], F32, sa))
        ebl = Buf(sb("ebl", [P, 8, 8], F32, sa))
        big = sb("big", [P, 32 * T], BF16, sa)
        qtT = Buf(big[:, 0:8 * T].rearrange("p (k t) -> p k t", t=T))
        ktT = Buf(big[:, 8 * T:16 * T].rearrange("p (k t) -> p k t", t=T))
        khT = Buf(big[:, 16 * T:24 * T].rearrange("p (k t) -> p k t", t=T))
        kh = Buf(big[:, 24 * T:32 * T].rearrange("p (s d) -> p s d", d=D))
        actT = big[:, :].rearrange("p (k t) -> p k t", t=T)
        act_slots = [qtT.s, ktT.s, khT.s, kh.s]
        vtm = Buf(sb("vtm", [P, NSUB, D], BF16, sa))
        gateT = Buf(sb("gateT", [P, 8, T], BF16, sa))
        ogT = Buf(sb("ogT", [P, 8, T], BF16, sa))
        S32t = sb("S32", [P, 8, P], F32, sa)
        Sbft = sb("Sbf", [P, 8, P], BF16, sa)
        S32s = [Slot("S32_%d" % i) for i in range(8)]
        Sbfs = [Slot("Sbf_%d" % i) for i in range(8)]
        scm = [Buf(sb("scm%d" % i, [P, 4, P], BF16, sa)) for i in range(2)]
        osq = Buf(sb("osq", [P, T], BF16, sa))
        ckn = Buf(sb("ckn", [P, 256], BF16, sa))
        cqn = Buf(sb("cqn", [P, 256], BF16, sa))
        krb = Buf(sb("krb", [P, P], BF16, sa))
        rt1 = Buf(sb("rt1", [P, 64], F32, sa))
        rt2 = Buf(sb("rt2", [P, 64], F32, sa))
        ss2 = Buf(sb("ss2", [P, 6], F32, sa))
        nscope = (junk2, ss_2, xh2, [7, 6])

        dma(SP, cmask.t[:, :], cmask_d.ap(), [], [cmask.s])
        dma(POOL, wsm.t[:, :, :], wsm_d.ap(), [], [wsm.s])
        mset(POOL, S32t[:, :, :], 0.0, S32s)
        mset(POOL, Sbft[:, :, :], 0.0, Sbfs)

        def proj_fm(wb, hl, pb):
            for kc in range(8):
                mm(pb.t[:, :], wb.t[:, kc * 512 + hl * P: kc * 512 + (hl + 1) * P], xnT.t[:, kc, :],
                   kc == 0, kc == 7, [wb.s, xnT.s], [pb.s], last=(kc == 7))

        hcount = 0
        _stop = int(_os.environ.get('DBG_STOP', 99))
        for ti in range(NT):
            t0 = ti * T
            for sub in range(NSUB):
                dma(SP, hs.t[:, sub, :], x_d.ap()[t0 + sub * P: t0 + (sub + 1) * P, :], [], [hs.s[sub]])
            if ti == 0:
                cast_chunks(0, CH_UP0)
            dma(SP, costm.t[:, :, :], costm_d.ap()[:, ti * NSUB:(ti + 1) * NSUB, :], [], [costm.s])
            dma(SP, sintm.t[:, :, :], sintm_d.ap()[:, ti * NSUB:(ti + 1) * NSUB, :], [], [sintm.s])
            for sub in range(NSUB):
                rmsnorm_T(nscope, hs, sub, [(xnT.t, [xnT.s], G_HGRN)])

            if _stop <= 1:
                break
            for c in range(2):
                wf = wload(wbufs, CH_F + c)
                wq = wload(wbufs, CH_Q + c)
                for hl in range(4):
                    hd = c * 4 + hl
                    pf, pq = ps[(2 * hd) % 6], ps[(2 * hd + 1) % 6]
                    ef_ = ef[hcount % 2]
                    eq_ = eq[hcount % 3]
                    hcount += 1
                    proj_fm(wf, hl, pf)
                    proj_fm(wq, hl, pq)
                    lb_ap = lbt.t[:, hd:hd + 1]
                    oml_ap = lbt.t[:, 8 + hd:9 + hd]
                    act(ef_.t[:, :], pf.t[:, :], AF.Exp, [pf.s], [ef_.s], scale=-1.0)
                    act(l1.t[:, :], ef_.t[:, :], AF.Ln, [ef_.s], [l1.s], bias=1.0)
                    act(l2.t[:, :], ef_.t[:, :], AF.Ln, [ef_.s, lbt.s], [l2.s], scale=lb_ap, bias=1.0)
                    act(sf.t[:, :], l1.t[:, :], AF.Exp, [l1.s], [sf.s], scale=-1.0)
                    tt(DVE, l2.t[:, :], l2.t[:, :], l1.t[:, :], ALU.subtract, [l1.s, l2.s], [l2.s])
                    stt(kk.t[:, :], ef_.t[:, :], oml_ap, sf.t[:, :], ALU.mult, ALU.mult, [ef_.s, lbt.s, sf.s], [kk.s])
                    DVE.op(lambda e: e.tensor_tensor_scan(out=bb.t[:, :], data0=cmask.t[:, :], data1=l2.t[:, :],
                                                          initial=0.0, op0=ALU.mult, op1=ALU.add),
                           reads=[l2.s, cmask.s], writes=[bb.s])
                    act(eb.t[:, :], bb.t[:, :], AF.Exp, [bb.s], [eb.s])
                    act(enb.t[:, :], bb.t[:, :], AF.Exp, [bb.s], [enb.s], scale=-1.0)
                    sigmoid_chain(pq, eq_)
                    tt(DVE, qs.t[:, :], pq.t[:, :], eq_.t[:, :], ALU.mult, [pq.s, eq_.s], [qs.s])
                    tt(DVE, qtT.t[:, hd, :], qs.t[:, :], eb.t[:, :], ALU.mult, [qs.s, eb.s], [qtT.s])
                    tt(DVE, kt32.t[:, :], kk.t[:, :], enb.t[:, :], ALU.mult, [kk.s, enb.s], [kt32.s])
                    cp(POOL, ktT.t[:, hd, :], kt32.t[:, :], [kt32.s], [ktT.s])
                    ebv = eb.t[:, :].rearrange("p (c t) -> p c t", t=64)[:, :, 63:64]
                    cp(POOL, ebl.t[:, hd, :].unsqueeze(2), ebv, [eb.s], [ebl.s])
                    tt(POOL, khT.t[:, hd, :].rearrange("p (c t) -> p c t", t=64),
                       kt32.t[:, :].rearrange("p (c t) -> p c t", t=64), ebv.to_broadcast([P, 8, 64]), ALU.mult,
                       [kt32.s, eb.s], [khT.s])
            if _stop <= 2:
                break
            if ti == 0:
                cast_chunks(CH_UP0, CH_WO1)
            for c in range(2):
                wg = wload(wbufs, CH_G + c)
                for hl in range(4):
                    hd = c * 4 + hl
                    pb = ps[hd % 6]
                    eq_ = eq[hcount % 3]
                    hcount += 1
                    proj_fm(wg, hl, pb)
                    sigmoid_chain(pb, eq_)
                    tt(DVE, gateT.t[:, hd, :], pb.t[:, :], eq_.t[:, :], ALU.mult, [pb.s, eq_.s], [gateT.s])

            if _stop <= 3:
                break
            for sub in range(NSUB):
                bi = 6 + (sub % 2)
                for hd in range(8):
                    tp(psb[bi][:, hd * P:(hd + 1) * P], khT.t[:, hd, sub * P:(sub + 1) * P], [khT.s], [ps[bi].s],
                       last=(hd == 7))
                act(kh.t[:, sub, :], psb[bi], AF.Copy, [ps[bi].s], [kh.s])

            if _stop <= 4:
                break
            for c in range(2):
                wi = wload(wbufs, CH_I + c)
                for sub in range(NSUB):
                    pb = ps[4 + (sub % 2)]
                    for kc in range(8):
                        mm(pb.t[:, :], xnT.t[:, kc, sub * P:(sub + 1) * P], wi.t[:, kc * 512:(kc + 1) * 512],
                           kc == 0, kc == 7, [wi.s, xnT.s], [pb.s], last=(kc == 7))
                    cp(DVE, vtm.t[:, sub, c * 512:(c + 1) * 512], pb.t[:, :], [pb.s], [vtm.s])

            if _stop <= 5:
                break
            for prr in range(NSUB):
                c0 = prr * P
                for g in range(2):
                    mset(DVE, ps[2 + g].t[:, :], 0.0, [ps[2 + g].s])
                for g in range(2):
                    for hl in range(4):
                        hd = g * 4 + hl
                        mm(ps[g].t[:, hl * P:(hl + 1) * P], ktT.t[:, hd, c0:c0 + P], qtT.t[:, hd, c0:c0 + P],
                           True, True, [ktT.s, qtT.s], [ps[g].s], last=(hl == 3))
                    tt(DVE, scm[g].t[:, :, :], ps[g].t[:, :].rearrange("p (h t) -> p h t", t=P),
                       bdmask.t[:, :].unsqueeze(1).to_broadcast([P, 4, P]), ALU.mult, [ps[g].s, bdmask.s], [scm[g].s])
                for half in range(2):
                    r0 = half * 64
                    cc0 = c0 + r0
                    ch = prr * 2 + half
                    for g in range(2):
                        po, pu = ps[2 + g], ps[4 + g]
                        for hl in range(4):
                            hd = g * 4 + hl
                            if half == 0:
                                mm(po.t[:, hl * P:(hl + 1) * P], vtm.t[:, prr, hd * P:(hd + 1) * P], scm[g].t[:, hl, :],
                                   False, False, [vtm.s, scm[g].s], [po.s], last=False, skip=True)
                            mm(po.t[:, hl * P + r0: hl * P + r0 + 64], Sbft[:, hd, :], qtT.t[:, hd, cc0:cc0 + 64],
                               False, (half == 1), [Sbfs[hd], qtT.s], [po.s], last=False, skip=True)
                            mm(pu.t[:, hl * P:(hl + 1) * P], kh.t[r0:r0 + 64, prr, hd * P:(hd + 1) * P],
                               vtm.t[r0:r0 + 64, prr, hd * P:(hd + 1) * P], True, True, [kh.s, vtm.s], [pu.s],
                               last=(hl == 3))
                        for hl in range(4):
                            hd = g * 4 + hl
                            stt(S32t[:, hd, :], S32t[:, hd, :], ebl.t[:, hd, ch:ch + 1], pu.t[:, hl * P:(hl + 1) * P],
                                ALU.mult, ALU.add, [S32s[hd], ebl.s, pu.s], [S32s[hd]])
                        cp(POOL, Sbft[:, g * 4:(g + 1) * 4, :], S32t[:, g * 4:(g + 1) * 4, :],
                           S32s[g * 4:(g + 1) * 4], Sbfs[g * 4:(g + 1) * 4])
                for g in range(2):
                    po, pm = ps[2 + g], ps[6 + g]
                    act(osq.t[:, :], po.t[:, :], AF.Square, [po.s], [osq.s])
                    mm(pm.t[:, :], onesdiv.t[:, :], osq.t[:, :], True, True, [onesdiv.s, osq.s], [pm.s])
                    act(l1.t[:, :], pm.t[:, :], AF.Ln, [pm.s, epsb.s], [l1.s], bias=epsb.t[:, 0:1])
                    act(l1.t[:, :], l1.t[:, :], AF.Exp, [l1.s], [l1.s], scale=-0.5)
                    stt(l2.t[:, :], po.t[:, :], gains.t[:, G_GN:G_GN + 1], l1.t[:, :], ALU.mult, ALU.mult,
                        [po.s, gains.s, l1.s], [l2.s])
                    tt(POOL, ogT.t[:, g * 4:(g + 1) * 4, c0:c0 + P], l2.t[:, :].rearrange("p (h t) -> p h t", t=P),
                       gateT.t[:, g * 4:(g + 1) * 4, c0:c0 + P], ALU.mult, [l2.s, gateT.s], [ogT.s])

            if _stop <= 6:
                break
            for c in range(2):
                wo = wload(wbufs, CH_O + c)
                for sub in range(NSUB):
                    pb = ps[sub]
                    for kc in range(8):
                        mm(pb.t[:, :], ogT.t[:, kc, sub * P:(sub + 1) * P], wo.t[:, kc * 512:(kc + 1) * 512],
                           kc == 0, kc == 7, [wo.s, ogT.s], [pb.s], last=(kc == 7))
                    hv = hs.t[:, sub, c * 512:(c + 1) * 512]
                    tt(DVE, hv, hv, pb.t[:, :], ALU.add, [pb.s, hs.s[sub]], [hs.s[sub]])

            if _stop <= 7:
                break
            mlp(hs, xnT, actT, act_slots, nscope, wbufs, ef, CH_UP0, CH_DN0, G_MLP0, [4, 5], [0, 1, 2, 3])

            if _stop <= 8:
                break
            if ti == 0:
                cast_chunks(CH_WO1, NCH)
            for sub in range(NSUB):
                ev = dma(POOL, h1_d.ap()[t0 + sub * P: t0 + (sub + 1) * P, :], hs.t[:, sub, :], [hs.s[sub]], [h1_slots[ti]])
                if upto == "A":
                    out_evs.append(ev)
            if _stop <= 9:
                break
            xn1T = ogT
            for sub in range(NSUB):
                rmsnorm_T(nscope, hs, sub, [(xnT.t, [xnT.s], G_KVIN), (xn1T.t, [xn1T.s], G_MLA)])
            if _stop <= 10:
                break
            for sub in range(NSUB):
                pk = ps[sub % 2]
                pq = ps[2 + (sub % 2)]
                for kc in range(8):
                    mm(pk.t[:, 0:384], xnT.t[:, kc, sub * P:(sub + 1) * P], wsm.t[:, kc, 0:384], kc == 0, kc == 7,
                       [wsm.s, xnT.s], [pk.s], last=(kc == 7))
                for kc in range(8):
                    mm(pq.t[:, 0:256], xn1T.t[:, kc, sub * P:(sub + 1) * P], wsm.t[:, kc, 384:640], kc == 0, kc == 7,
                       [wsm.s, xn1T.s], [pq.s], last=(kc == 7))
                if _stop <= 11:
                    break
                act(junk.t[:, 0:256], pk.t[:, 0:256], AF.Square, [pk.s], [junk.s, ss2.s], accum=ss2.t[:, 0:1])
                act(junk.t[:, 256:512], pq.t[:, 0:256], AF.Square, [pq.s], [junk.s, ss2.s], accum=ss2.t[:, 1:2])
                act(ss2.t[:, 2:4], ss2.t[:, 0:2], AF.Ln, [ss2.s, epsb.s], [ss2.s], scale=1.0 / 256, bias=epsb.t[:, 0:1])
                act(ss2.t[:, 4:6], ss2.t[:, 2:4], AF.Exp, [ss2.s], [ss2.s], scale=-0.5)
                ts(DVE, ckn.t[:, :], pk.t[:, 0:256], ss2.t[:, 4:5], None, ALU.mult, None, [pk.s, ss2.s], [ckn.s])
                ts(DVE, cqn.t[:, :], pq.t[:, 0:256], ss2.t[:, 5:6], None, ALU.mult, None, [pq.s, ss2.s], [cqn.s])
                if _stop <= 12:
                    break
                tt(DVE, rt1.t[:, :], pk.t[:, 256:320], costm.t[:, sub, :], ALU.mult, [pk.s, costm.s], [rt1.s])
                tt(DVE, rt2.t[:, :], pk.t[:, 320:384], sintm.t[:, sub, :], ALU.mult, [pk.s, sintm.s], [rt2.s])
                tt(POOL, krb.t[:, 0:64], rt1.t[:, :], rt2.t[:, :], ALU.add, [rt1.s, rt2.s], [krb.s])
                tt(POOL, krb.t[:, 64:128], rt1.t[:, :], rt2.t[:, :], ALU.add, [rt1.s, rt2.s], [krb.s])
                if _stop <= 13:
                    break
                bi = 4 + (sub % 2)
                ptv = psb[bi]
                for cc in range(2):
                    tp(ptv[:, cc * P:(cc + 1) * P], ckn.t[:, cc * P:(cc + 1) * P], [ckn.s], [ps[bi].s], last=False)
                for cc in range(2):
                    tp(ptv[:, (2 + cc) * P:(3 + cc) * P], cqn.t[:, cc * P:(cc + 1) * P], [cqn.s], [ps[bi].s], last=False)
                tp(ptv[:, 4 * P:5 * P], krb.t[:, :], [krb.s], [ps[bi].s], last=True)
                if _stop <= 14:
                    break
                tsl = slice(t0 + sub * P, t0 + (sub + 1) * P)
                for cc in range(2):
                    ts(DVE, ckvT.t[:, cc, tsl], ptv[:, cc * P:(cc + 1) * P], gains.t[:, G_KVN + cc:G_KVN + cc + 1], None,
                       ALU.mult, None, [ps[bi].s, gains.s], [ckvT.s])
                    ts(DVE, cqT.t[:, cc, tsl], ptv[:, (2 + cc) * P:(3 + cc) * P], gains.t[:, G_QN + cc:G_QN + cc + 1], None,
                       ALU.mult, None, [ps[bi].s, gains.s], [cqT.s])
                if not _os.environ.get('DBG_NOKR'):
                    cp(DVE, kropeT.t[:, tsl], ptv[:, 4 * P:5 * P], [ps[bi].s], [kropeT.s])
            if _stop <= 15:
                break
        if "c" in dbgset:
            for (nm, src_) in (("dbg_ckvT", ckvT), ("dbg_cqT", cqT)):
                dd = dram(nm, [P, 2, S], BF16, "ExternalOutput")
                out_evs.append(dma(POOL, dd.ap(), src_.t[:, :, :], [src_.s], []))
            dd = dram("dbg_kropeT", [64, S], BF16, "ExternalOutput")
            out_evs.append(dma(POOL, dd.ap(), kropeT.t[0:64, :], [kropeT.s], []))
        pg.flush(final_waits=out_evs if upto == "A" else ())
    if upto == "A":
        return nc, es

    with ExitStack() as sbs:
        cosfm = Buf(sb("cosfm", [P, S], F32, sbs))
        sinfm = Buf(sb("sinfm", [P, S], F32, sbs))
        whb = [Buf(sb("whb%d" % i, [P, 2, 640], BF16, sbs)) for i in range(2)]
        KhT = [Buf(sb("KhT%d" % i, [P, S], BF16, sbs)) for i in range(2)]
        QnT = [Buf(sb("QnT%d" % i, [P, S], BF16, sbs)) for i in range(2)]
        QrT = [Buf(sb("QrT%d" % i, [P, S], BF16, sbs)) for i in range(2)]
        Vh = [Buf(sb("Vh%d" % i, [P, NTT, P], BF16, sbs)) for i in range(2)]
        pT = [Buf(sb("pT%d" % i, [P, 512], BF16, sbs)) for i in range(6)]
        q1 = [Buf(sb("q1_%d" % i, [P, 512], F32, sbs)) for i in range(2)]
        q2 = [Buf(sb("q2_%d" % i, [P, 512], F32, sbs)) for i in range(2)]
        accD = [Buf(sb("accD%d" % i, [P, 512], F32, sbs)) for i in range(2)]
        accP = [Buf(sb("accP%d" % i, [P, 512], F32, sbs)) for i in range(2)]
        rc = [Buf(sb("rc%d" % i, [P, 512], F32, sbs)) for i in range(2)]
        ost = [Buf(sb("ost%d" % i, [P, 512], BF16, sbs)) for i in range(2)]
        dma(SP, cosfm.t[:, :], cosfm_d.ap(), [], [cosfm.s])
        dma(SP, sinfm.t[:, :], sinfm_d.ap(), [], [sinfm.s])
        evac_rr = [0]

        def evac(dst_ap, src_ap, reads, writes):
            i = evac_rr[0]
            evac_rr[0] += 1
            if i % 2 == 0:
                act(dst_ap, src_ap, AF.Copy, reads, writes)
            else:
                cp(DVE, dst_ap, src_ap, reads, writes)

        def prep_groups(h):
            r = h % 2
            wh_ = whb[r]
            pb = ps[7]
            groups = []
            groups.append(lambda: dma(POOL, wh_.t[:, :, :], wh_d.ap()[h], [], [wh_.s]))

            def proj(c0_, src_, tsl):
                for cc in range(2):
                    mm(pb.t[:, :], wh_.t[:, cc, c0_:c0_ + 128], src_.t[:, cc, tsl], cc == 0, cc == 1,
                       [wh_.s, src_.s], [pb.s], last=(cc == 1))

            for tb in range(NTB):
                tsl = slice(tb * 512, (tb + 1) * 512)
                rr = tb % 2

                def gK(tsl=tsl):
                    proj(0, ckvT, tsl)
                    evac(KhT[r].t[:, tsl], pb.t[:, :], [pb.s], [KhT[r].s])

                def gQ(tsl=tsl):
                    proj(256, cqT, tsl)
                    evac(QnT[r].t[:, tsl], pb.t[:, :], [pb.s], [QnT[r].s])

                def gA(tsl=tsl, rr=rr):
                    proj(384, cqT, tsl)
                    tt(DVE, q1[rr].t[:, :], pb.t[:, :], cosfm.t[:, tsl], ALU.mult, [pb.s, cosfm.s], [q1[rr].s])

                def gB(tsl=tsl, rr=rr):
                    proj(512, cqT, tsl)
                    tt(DVE, q2[rr].t[:, :], pb.t[:, :], sinfm.t[:, tsl], ALU.mult, [pb.s, sinfm.s], [q2[rr].s])
                    tt(POOL, QrT[r].t[:, tsl], q1[rr].t[:, :], q2[rr].t[:, :], ALU.add, [q1[rr].s, q2[rr].s], [QrT[r].s])

                groups += [gK, gQ, gA, gB]
            for g4 in range(NTT // 4):
                def gV(g4=g4):
                    for i4 in range(4):
                        tt_ = g4 * 4 + i4
                        for cc in range(2):
                            mm(pb.t[:, i4 * P:(i4 + 1) * P], ckvT.t[:, cc, tt_ * P:(tt_ + 1) * P], wh_.t[:, cc, 128:256],
                               cc == 0, cc == 1, [wh_.s, ckvT.s], [pb.s], last=(cc == 1 and i4 == 3))
                    evac(Vh[r].t[:, g4 * 4:(g4 + 1) * 4, :], pb.t[:, :].rearrange("p (a b) -> p a b", b=P), [pb.s], [Vh[r].s])
                groups.append(gV)
            return groups

        qbi = 0
        ptc = 0
        pend_epi = []
        for g_ in prep_groups(0):
            g_()
        for h in range(NH):
            r = h % 2
            pend = prep_groups(h + 1) if h + 1 < NH else []
            for qb in range(NTB):
                po = ps[4 + (qbi % 2)]
                pz = ps[6]
                aD, aP = accD[qbi % 2], accP[qbi % 2]
                rr = qbi % 2
                qbi += 1
                nkt = 4 * (qb + 1)
                tiles = []
                for kt in range(nkt):
                    q0 = 0 if kt < 4 * qb else (kt - 4 * qb) * P
                    tiles.append((kt, q0))
                mset(POOL, aP.t[:, :], 0.0, [aP.s])

                def qk_pair(pi, tiles=tiles, qb=qb, r=r):
                    info = []
                    for j in range(2):
                        idx = 2 * pi + j
                        kt, q0 = tiles[idx]
                        info.append((kt, ps[idx % 4], 512 - q0, slice(qb * 512 + q0, (qb + 1) * 512), j))
                    ndiag = sum(1 for (kt, _, _, _, _) in info if kt >= 4 * qb)
                    for (kt, pss, n, qsl, j) in info:
                        mm(pss.t[:, 0:n], KhT[r].t[:, kt * P:(kt + 1) * P], QnT[r].t[:, qsl], True, False,
                           [KhT[r].s, QnT[r].s], [pss.s], last=False)
                    for (kt, pss, n, qsl, j) in info:
                        mm(pss.t[:, 0:n], kropeT.t[j * 64:(j + 1) * 64, kt * P:(kt + 1) * P],
                           QrT[r].t[j * 64:(j + 1) * 64, qsl], False, (kt < 4 * qb),
                           [kropeT.s, QrT[r].s], [pss.s], last=(j == 1 and ndiag == 0))
                    k = 0
                    for (kt, pss, n, qsl, j) in info:
                        if kt >= 4 * qb:
                            k += 1
                            mm(pss.t[:, 0:P], ident.t[:, :], negmask.t[:, :], False, True, [ident.s, negmask.s], [pss.s],
                               last=(k == ndiag))

                def pv_(idx, tiles=tiles, po=po, aD=aD, aP=aP, nkt=nkt, r=r):
                    nonlocal ptc
                    kt, q0 = tiles[idx]
                    pss = ps[idx % 4]
                    pt_ = pT[ptc % len(pT)]
                    ptc += 1
                    n = 512 - q0
                    act(pt_.t[:, 0:n], pss.t[:, 0:n], AF.Exp, [pss.s], [pt_.s], scale=ATT_SCALE)
                    lastk = (idx == nkt - 1)
                    mm(po.t[:, q0:512], Vh[r].t[:, kt, :], pt_.t[:, 0:n], idx == 0, lastk, [Vh[r].s, pt_.s], [po.s],
                       last=True, skip=True)
                    if idx == 0:
                        cp(DVE, aD.t[:, :], pt_.t[:, :], [pt_.s], [aD.s])
                    elif idx % 3 == 2:
                        tt(POOL, aP.t[:, q0:512], aP.t[:, q0:512], pt_.t[:, 0:n], ALU.add, [aP.s, pt_.s], [aP.s])
                    else:
                        tt(DVE, aD.t[:, q0:512], aD.t[:, q0:512], pt_.t[:, 0:n], ALU.add, [aD.s, pt_.s], [aD.s])

                def epilogue(po=po, pz=pz, aD=aD, aP=aP, rr=rr, h=h, qb=qb):
                    tt(DVE, aD.t[:, :], aD.t[:, :], aP.t[:, :], ALU.add, [aD.s, aP.s], [aD.s])
                    mm(pz.t[:, :], ones32.t[:, :], aD.t[:, :], True, True, [ones32.s, aD.s], [pz.s])
                    act(rc[rr].t[:, :], pz.t[:, :], AF.Ln, [pz.s], [rc[rr].s])
                    act(rc[rr].t[:, :], rc[rr].t[:, :], AF.Exp, [rc[rr].s], [rc[rr].s], scale=-1.0)
                    tt(DVE, ost[rr].t[:, :], po.t[:, :], rc[rr].t[:, :], ALU.mult, [po.s, rc[rr].s], [ost[rr].s])
                    ev = dma(POOL, o_d.ap()[h, :, qb * 512:(qb + 1) * 512], ost[rr].t[:, :], [ost[rr].s], [o_slots[h][qb]])
                    if upto == "B":
                        out_evs.append(ev)

                npair = nkt // 2
                qk_pair(0)
                for pi in range(npair):
                    if pi + 1 < npair:
                        qk_pair(pi + 1)
                    pv_(2 * pi)
                    pv_(2 * pi + 1)
                    if pi == 0 and pend_epi:
                        pend_epi.pop(0)()
                    if pend:
                        pend.pop(0)()
                pend_epi.append(epilogue)
            while pend:
                pend.pop(0)()
        while pend_epi:
            pend_epi.pop(0)()
        pg.flush(final_waits=out_evs if upto == "B" else ())
    if upto == "B":
        return nc, es

    with ExitStack() as sc_:
        hsb2 = [HS(sb("hsc%d" % i, [P, NSUB, D], F32, sc_), "hsc%d" % i) for i in range(2)]
        oT = [Buf(sb("oTc%d" % i, [P, NH, T], BF16, sc_)) for i in range(2)]
        xnT = Buf(sb("xnTc", [P, 8, T], BF16, sc_))
        junk2 = [Buf(sb("junkc%d" % i, [P, D], BF16, sc_)) for i in range(2)]
        ss_2 = [Buf(sb("ssc%d" % i, [P, 4], F32, sc_)) for i in range(2)]
        xh2 = [Buf(sb("xhc%d" % i, [P, D], BF16, sc_)) for i in range(2)]
        wbufs = [Buf(sb("wbufc%d" % i, [P, 4096], BF16, sc_)) for i in range(4)]
        actTb = Buf(sb("actTc", [P, 32, T], BF16, sc_))
        sqb = [Buf(sb("sqbc%d" % i, [P, T], F32, sc_)) for i in range(2)]
        fnb = Buf(sb("fnb", [P, D], F32, sc_))
        outb = [Buf(sb("outb%d" % i, [P, D], F32, sc_)) for i in range(2)]
        nscope = (junk2, ss_2, xh2, [7, 6])
        dma(SP, fnb.t[:, :], fnorm_d.ap(), [], [fnb.s])
        for ti in range(NT):
            ot = oT[ti % 2]
            hs = hsb2[ti % 2]
            t0 = ti * T
            for h4 in range(NH // 4):
                dma(SP, ot.t[:, h4 * 4:(h4 + 1) * 4, :], o_d.ap()[h4 * 4:(h4 + 1) * 4, :, t0:t0 + T].rearrange("h p t -> p h t"),
                    [o_slots[h][ti] for h in range(h4 * 4, (h4 + 1) * 4)], [ot.s])
            for sub in range(NSUB):
                dma(SP, hs.t[:, sub, :], h1_d.ap()[t0 + sub * P: t0 + (sub + 1) * P, :], [h1_slots[ti]], [hs.s[sub]])
            for j in range(2):
                for i in range(2):
                    wb = wload(wbufs, CH_WO1 + j * 2 + i)
                    for sub in range(NSUB):
                        pb = ps[sub]
                        for kc in range(8):
                            mm(pb.t[:, :], ot.t[:, i * 8 + kc, sub * P:(sub + 1) * P], wb.t[:, kc * 512:(kc + 1) * 512],
                               (i == 0 and kc == 0), (i == 1 and kc == 7), [wb.s, ot.s], [pb.s], last=(kc == 7))
                for sub in range(NSUB):
                    pb = ps[sub]
                    hv = hs.t[:, sub, j * 512:(j + 1) * 512]
                    tt(DVE, hv, hv, pb.t[:, :], ALU.add, [pb.s, hs.s[sub]], [hs.s[sub]])
            mlp(hs, xnT, actTb.t[:, :, :], [actTb.s], nscope, wbufs, sqb, CH_UP1, CH_DN1, G_MLP1, [4, 5], [0, 1, 2, 3])
            for sub in range(NSUB):
                ob = outb[sub % 2]
                hsl = hs.t[:, sub, :]
                junk, ss = junk2[sub % 2], ss_2[sub % 2]
                act(junk.t[:, :], hsl, AF.Square, [hs.s[sub]], [junk.s, ss.s], accum=ss.t[:, 0:1])
                act(ss.t[:, 1:2], ss.t[:, 0:1], AF.Ln, [ss.s, epsb.s], [ss.s], scale=1.0 / D, bias=epsb.t[:, 0:1])
                act(ss.t[:, 2:3], ss.t[:, 1:2], AF.Exp, [ss.s], [ss.s], scale=-0.5)
                stt(ob.t[:, :], hsl, ss.t[:, 2:3], fnb.t[:, :], ALU.mult, ALU.mult, [hs.s[sub], ss.s, fnb.s], [ob.s])
                ev = dma(POOL, out_d.ap()[t0 + sub * P: t0 + (sub + 1) * P, :], ob.t[:, :], [ob.s], [])
                out_evs.append(ev)
        pg.flush(final_waits=out_evs)
    return nc, es


_CACHE = {}


def kernel(**inputs):
    S = inputs["x"].shape[1]
    ncores = inputs["x"].shape[0]
    inp = {k: np.asarray(v) for k, v in inputs.items()}
    packed = _pack_weights(inp)
    consts = _consts(S)
    nc, es = build(S=S, upto="C", debug=False)
    in_maps = []
    for b in range(ncores):
        m = {"x": np.ascontiguousarray(inp["x"][b], dtype=np.float32)}
        m.update(packed)
        m.update(consts)
        in_maps.append(m)
    res = run_bass_kernel_spmd(nc, in_maps, core_ids=list(range(ncores)))
    es.close()
    out = np.stack([np.asarray(r["out"]) for r in res.results], axis=0).astype(np.float32)
    return out
```
